# Optimizing a Trainium2 kernel written in Bass

```python
import jax, jax.numpy as jnp
from jax import lax
import numpy as np

D_MODEL = 2048
BATCH = 4
SEQ = 8192
DEPTH = 1

LRU_WIDTH = 2048
LRU_BLOCKS = 16
LRU_BLOCK_W = LRU_WIDTH // LRU_BLOCKS
CONV_WIDTH = 4
LRU_C = 8.0
N_HEADS = 16
N_KV_GROUPS = 4
HEADS_PER_GROUP = N_HEADS // N_KV_GROUPS
HEAD_DIM = 128
CMP_STRIDE = 16
CMP_BLOCK = 2 * CMP_STRIDE
CMP_HIDDEN = 256
SLC_BLOCK = 64
N_SELECT = 16
WINDOW = 512
Q_BLOCK = 128
SLC_CHUNK = 16
D_FF = 4 * D_MODEL
PLE_DIM = 256
EPS = 1e-6
NEG = -1e30
FORCED_SCORE = 1e4

Q_W = N_HEADS * HEAD_DIM
KV_W = N_KV_GROUPS * HEAD_DIM
N_NSA_GATES = 3 * N_HEADS
IN_SPLITS = (LRU_WIDTH, LRU_WIDTH, Q_W, 6 * KV_W, N_NSA_GATES, D_MODEL, D_MODEL)
D_IN = sum(IN_SPLITS)

kernel_name = "hybrid_rglru_nsa_gated_block"


def rms_norm(x, g):
    xf = x.astype(jnp.float32)
    y = xf * lax.rsqrt(jnp.mean(xf * xf, axis=-1, keepdims=True) + EPS)
    return (y * g.astype(jnp.float32)).astype(x.dtype)


def masked_softmax(s, mask):
    s = jnp.where(mask, s.astype(jnp.float32), NEG)
    m = jnp.max(s, axis=-1, keepdims=True)
    e = jnp.where(mask, jnp.exp(s - m), 0.0)
    return e / jnp.maximum(jnp.sum(e, axis=-1, keepdims=True), 1e-30)


def causal_depthwise_conv(x, w, b):
    y = lax.conv_general_dilated(x, w[:, None, :].astype(x.dtype), window_strides=(1,),
                                 padding=[(CONV_WIDTH - 1, 0)],
                                 dimension_numbers=('NWC', 'WIO', 'NWC'),
                                 feature_group_count=x.shape[-1])
    return y + b


def block_diag_linear(x, w, b):
    B_, S_, _ = x.shape
    xb = x.reshape(B_, S_, LRU_BLOCKS, LRU_BLOCK_W)
    return (jnp.einsum('bsnc,ncd->bsnd', xb, w) + b).reshape(B_, S_, LRU_WIDTH)


def rg_lru(x, w_a, b_a, w_x, b_x, lam):
    r = jax.nn.sigmoid(block_diag_linear(x, w_a, b_a)).astype(jnp.float32)
    i = jax.nn.sigmoid(block_diag_linear(x, w_x, b_x))
    log_a = -LRU_C * r * jax.nn.softplus(-lam.astype(jnp.float32))
    a = jnp.exp(log_a)
    mult = jnp.sqrt(jnp.maximum(-jnp.expm1(2.0 * log_a), 0.0))
    bt = mult * (i * x).astype(jnp.float32)

    def combine(left, right):
        a_l, b_l = left
        a_r, b_r = right
        return a_l * a_r, a_r * b_l + b_r

    _, h = lax.associative_scan(combine, (a, bt), axis=1)
    return h.astype(x.dtype)


def compress_tokens(k, pos, w1, w2):
    B_, S_ = k.shape[:2]
    kr = k.reshape(B_, S_ // CMP_STRIDE, CMP_STRIDE, N_KV_GROUPS, HEAD_DIM)
    blocks = jnp.concatenate([kr[:, :-1], kr[:, 1:]], axis=2)
    blocks = blocks + pos[None, None, :, None, :]
    flat = jnp.moveaxis(blocks, 3, 2).reshape(B_, -1, N_KV_GROUPS, CMP_BLOCK * HEAD_DIM)
    return jax.nn.gelu(flat @ w1) @ w2


def nsa_attention(q, k_cmp, v_cmp, k_slc, v_slc, k_win, v_win, gates,
                  cmp_pos_k, cmp_w1_k, cmp_w2_k, cmp_pos_v, cmp_w1_v, cmp_w2_v):
    B_, S_ = q.shape[:2]
    G, HPG, DK = N_KV_GROUPS, HEADS_PER_GROUP, HEAD_DIM
    q = q.reshape(B_, S_, G, HPG, DK) * (DK ** -0.5)
    kc = compress_tokens(k_cmp, cmp_pos_k, cmp_w1_k, cmp_w2_k)
    vc = compress_tokens(v_cmp, cmp_pos_v, cmp_w1_v, cmp_w2_v)
    n_cmp = S_ // CMP_STRIDE - 1
    n_sb = S_ // SLC_BLOCK
    n_sel = min(N_SELECT, n_sb)
    c_start = jnp.arange(n_cmp) * CMP_STRIDE
    cmp_end = c_start + CMP_BLOCK - 1
    s_start = jnp.arange(n_sb) * SLC_BLOCK
    overlap = ((c_start[:, None] < s_start[None, :] + SLC_BLOCK) &
               (s_start[None, :] < c_start[:, None] + CMP_BLOCK)).astype(jnp.float32)
    ks_blocks = jnp.moveaxis(k_slc.reshape(B_, n_sb, SLC_BLOCK, G, DK), 3, 1)
    vs_blocks = jnp.moveaxis(v_slc.reshape(B_, n_sb, SLC_BLOCK, G, DK), 3, 1)
    pad = ((0, 0), (WINDOW, 0), (0, 0), (0, 0))
    kw = jnp.pad(k_win, pad)
    vw = jnp.pad(v_win, pad)
    span = WINDOW + Q_BLOCK
    n_chunk = Q_BLOCK // SLC_CHUNK
    blk_j = jnp.arange(n_sb)

    def selected_attention(qb, sel, t):
        q_c = jnp.moveaxis(qb.reshape(B_, n_chunk, SLC_CHUNK, G, HPG, DK), 1, 0)
        sel_c = jnp.moveaxis(sel.reshape(B_, G, n_chunk, SLC_CHUNK, n_sel), 2, 0)
        t_c = t.reshape(n_chunk, SLC_CHUNK)

        def chunk(args):
            qc, ic, tc = args
            gather = jax.vmap(jax.vmap(lambda kb, ib: kb[ib]))
            kg = gather(ks_blocks, ic).reshape(B_, G, SLC_CHUNK, n_sel * SLC_BLOCK, DK)
            vg = gather(vs_blocks, ic).reshape(B_, G, SLC_CHUNK, n_sel * SLC_BLOCK, DK)
            pos = (ic[..., None] * SLC_BLOCK + jnp.arange(SLC_BLOCK)).reshape(B_, G, SLC_CHUNK, n_sel * SLC_BLOCK)
            s = jnp.einsum('bcghd,bgckd->bghck', qc, kg)
            mask = (pos <= tc[None, None, :, None])[:, :, None]
            pr = masked_softmax(s, mask)
            return jnp.einsum('bghck,bgckd->bcghd', pr.astype(vg.dtype), vg)

        o = lax.map(chunk, (q_c, sel_c, t_c))
        return jnp.moveaxis(o, 0, 1).reshape(B_, Q_BLOCK, G, HPG, DK)

    def attend_block(blk):
        s0 = blk * Q_BLOCK
        t = s0 + jnp.arange(Q_BLOCK)
        qb = lax.dynamic_slice_in_dim(q, s0, Q_BLOCK, axis=1)
        sc = jnp.einsum('bqghd,bcgd->bghqc', qb, kc)
        pc = masked_softmax(sc, cmp_end[None, :] <= t[:, None])
        o_cmp = jnp.einsum('bghqc,bcgd->bqghd', pc.astype(vc.dtype), vc)
        imp = jnp.einsum('bghqc,cn->bgqn', pc, overlap)
        cur = t // SLC_BLOCK
        forced = (blk_j[None, :] == 0) | (blk_j[None, :] == cur[:, None]) | (blk_j[None, :] == cur[:, None] - 1)
        valid = blk_j[None, :] <= cur[:, None]
        imp = jnp.where(forced, FORCED_SCORE, jnp.where(valid, imp, -FORCED_SCORE))
        _, sel = lax.top_k(imp, n_sel)
        o_slc = selected_attention(qb, sel, t)
        kwb = lax.dynamic_slice_in_dim(kw, s0, span, axis=1)
        vwb = lax.dynamic_slice_in_dim(vw, s0, span, axis=1)
        pos_w = s0 - WINDOW + jnp.arange(span)
        diff = t[:, None] - pos_w[None, :]
        mask_w = (diff >= 0) & (diff < WINDOW) & (pos_w[None, :] >= 0)
        sw = jnp.einsum('bqghd,bkgd->bghqk', qb, kwb)
        pw = masked_softmax(sw, mask_w)
        o_win = jnp.einsum('bghqk,bkgd->bqghd', pw.astype(vwb.dtype), vwb)
        gb = lax.dynamic_slice_in_dim(gates, s0, Q_BLOCK, axis=1)
        return (gb[:, :, 0, :, :, None] * o_cmp + gb[:, :, 1, :, :, None] * o_slc
                + gb[:, :, 2, :, :, None] * o_win)

    o = lax.map(attend_block, jnp.arange(S_ // Q_BLOCK))
    return jnp.moveaxis(o, 0, 1).reshape(B_, S_, Q_W)


def setup_inputs(seed: int = 0) -> dict:
    key = jax.random.key(seed)
    ks = jax.random.split(key, 32)
    f32 = jnp.float32

    def nrm(k, shape, scale):
        return jax.random.normal(k, shape, f32) * scale

    def gain(k, shape):
        return 1.0 + 0.01 * jax.random.normal(k, shape, f32)

    L = DEPTH
    u = jax.random.uniform(ks[8], (L, LRU_WIDTH), f32, 0.9, 0.999)
    a0 = u ** (1.0 / LRU_C)
    lru_lambda = jnp.log(a0) - jnp.log1p(-a0)
    return {
        "x": nrm(ks[0], (BATCH, SEQ, D_MODEL), 1.0),
        "p": nrm(ks[1], (DEPTH, BATCH, SEQ, PLE_DIM), 1.0),
        "norm_mix_g": gain(ks[2], (L, D_MODEL)),
        "w_in": nrm(ks[3], (L, D_MODEL, D_IN), D_MODEL ** -0.5),
        "b_in": nrm(ks[4], (L, D_IN), 0.01),
        "conv_w": nrm(ks[5], (L, CONV_WIDTH, LRU_WIDTH), CONV_WIDTH ** -0.5),
        "conv_b": nrm(ks[6], (L, LRU_WIDTH), 0.01),
        "w_gate_a": nrm(ks[7], (L, LRU_BLOCKS, LRU_BLOCK_W, LRU_BLOCK_W), LRU_BLOCK_W ** -0.5),
        "b_gate_a": nrm(ks[9], (L, LRU_BLOCKS, LRU_BLOCK_W), 0.01),
        "w_gate_x": nrm(ks[10], (L, LRU_BLOCKS, LRU_BLOCK_W, LRU_BLOCK_W), LRU_BLOCK_W ** -0.5),
        "b_gate_x": nrm(ks[11], (L, LRU_BLOCKS, LRU_BLOCK_W), 0.01),
        "lru_lambda": lru_lambda,
        "w_lru_out": nrm(ks[12], (L, LRU_WIDTH, D_MODEL), LRU_WIDTH ** -0.5),
        "cmp_pos_k": nrm(ks[13], (L, CMP_BLOCK, HEAD_DIM), 0.1),
        "cmp_w1_k": nrm(ks[14], (L, CMP_BLOCK * HEAD_DIM, CMP_HIDDEN), (CMP_BLOCK * HEAD_DIM) ** -0.5),
        "cmp_w2_k": nrm(ks[15], (L, CMP_HIDDEN, HEAD_DIM), CMP_HIDDEN ** -0.5),
        "cmp_pos_v": nrm(ks[16], (L, CMP_BLOCK, HEAD_DIM), 0.1),
        "cmp_w1_v": nrm(ks[17], (L, CMP_BLOCK * HEAD_DIM, CMP_HIDDEN), (CMP_BLOCK * HEAD_DIM) ** -0.5),
        "cmp_w2_v": nrm(ks[18], (L, CMP_HIDDEN, HEAD_DIM), CMP_HIDDEN ** -0.5),
        "w_attn_out": nrm(ks[19], (L, Q_W, D_MODEL), Q_W ** -0.5),
        "w_o": nrm(ks[20], (L, D_MODEL, D_MODEL), D_MODEL ** -0.5),
        "norm_mlp_g": gain(ks[21], (L, D_MODEL)),
        "w_up": nrm(ks[22], (L, D_MODEL, D_FF), D_MODEL ** -0.5),
        "w_down": nrm(ks[23], (L, D_FF, D_MODEL), D_FF ** -0.5),
        "norm_ple_g": gain(ks[24], (L, D_MODEL)),
        "w_ple_gate": nrm(ks[25], (L, D_MODEL, D_MODEL), D_MODEL ** -0.5),
        "w_ple_proj": nrm(ks[26], (L, PLE_DIM, D_MODEL), PLE_DIM ** -0.5),
        "norm_final_g": gain(ks[27], (D_MODEL,)),
    }


def reference(x, p, norm_mix_g, w_in, b_in, conv_w, conv_b, w_gate_a, b_gate_a, w_gate_x, b_gate_x,
              lru_lambda, w_lru_out, cmp_pos_k, cmp_w1_k, cmp_w2_k, cmp_pos_v, cmp_w1_v, cmp_w2_v,
              w_attn_out, w_o, norm_mlp_g, w_up, w_down, norm_ple_g, w_ple_gate, w_ple_proj,
              norm_final_g):
    B_, S_, _ = x.shape
    split_points = [int(v) for v in np.cumsum(IN_SPLITS)[:-1]]
    for i in range(DEPTH):
        h = rms_norm(x, norm_mix_g[i])
        u = h @ w_in[i] + b_in[i]
        u_lx, u_ly, u_q, u_kv, u_g, u_ma, u_mb = jnp.split(u, split_points, axis=-1)
        xc = causal_depthwise_conv(u_lx, conv_w[i], conv_b[i])
        hl = rg_lru(xc, w_gate_a[i], b_gate_a[i], w_gate_x[i], b_gate_x[i], lru_lambda[i])
        y_a = (jax.nn.gelu(u_ly) * hl) @ w_lru_out[i]
        k_cmp, v_cmp, k_slc, v_slc, k_win, v_win = [
            t.reshape(B_, S_, N_KV_GROUPS, HEAD_DIM) for t in jnp.split(u_kv, 6, axis=-1)]
        gates = jax.nn.sigmoid(u_g).reshape(B_, S_, 3, N_KV_GROUPS, HEADS_PER_GROUP)
        o_nsa = nsa_attention(u_q, k_cmp, v_cmp, k_slc, v_slc, k_win, v_win, gates,
                              cmp_pos_k[i], cmp_w1_k[i], cmp_w2_k[i],
                              cmp_pos_v[i], cmp_w1_v[i], cmp_w2_v[i])
        y_b = o_nsa @ w_attn_out[i]
        mixed = jax.nn.sigmoid(u_ma) * y_a + jax.nn.sigmoid(u_mb) * y_b
        x = x + mixed @ w_o[i]
        h = rms_norm(x, norm_mlp_g[i])
        x = x + jnp.square(jax.nn.relu(h @ w_up[i])) @ w_down[i]
        h = rms_norm(x, norm_ple_g[i])
        x = x + jax.nn.sigmoid(h @ w_ple_gate[i]) * (p[i] @ w_ple_proj[i])
    return rms_norm(x, norm_final_g)
```

```python
import numpy as np
import concourse.bass as bass
import concourse.mybir as mybir
from concourse.bass_utils import run_bass_kernel_spmd

F32 = mybir.dt.float32
BF16 = mybir.dt.bfloat16
AF = mybir.ActivationFunctionType
ALU = mybir.AluOpType
AX = mybir.AxisListType


class Buf:
    __slots__ = ("w", "r", "name")

    def __init__(self, name=""):
        self.w = {}
        self.r = {}
        self.name = name


class TT:
    def __init__(self, t, name=""):
        self.t = t
        self.buf = Buf(name)

    def __getitem__(self, k):
        return self.t[k]


class FW:
    def __init__(self, nc, ndma_sems=6):
        self.nc = nc
        self.eng = {"pe": nc.tensor, "act": nc.scalar, "dve": nc.vector, "pool": nc.gpsimd, "sp": nc.sync}
        self.sem = {k: nc.alloc_semaphore("s_" + k) for k in self.eng}
        self.cnt = {k: 0 for k in self.eng}
        self.waited = {k: {} for k in self.eng}
        self.semobj = {}
        for k, s in self.sem.items():
            self.semobj[id(s)] = s
        self.dq = {}
        for q in ("sp", "pool", "act"):
            sems = [nc.alloc_semaphore("d_%s%d" % (q, i)) for i in range(ndma_sems)]
            for s in sems:
                self.semobj[id(s)] = s
            self.dq[q] = {"sems": sems, "n": 0, "tgt": [0] * ndma_sems}
        self.ninst = 0

    def _wait(self, e, semid, val):
        if val <= 0:
            return
        w = self.waited[e]
        if w.get(semid, 0) >= val:
            return
        self.eng[e].wait_ge(self.semobj[semid], val)
        w[semid] = val

    def _deps(self, e, reads, writes, skip_self=False):
        me = id(self.sem[e])
        for b in reads:
            b = b.buf if isinstance(b, TT) else b
            for s, v in b.w.items():
                if skip_self and s == me:
                    continue
                self._wait(e, s, v)
        for b in writes:
            b = b.buf if isinstance(b, TT) else b
            for s, v in b.w.items():
                if skip_self and s == me:
                    continue
                self._wait(e, s, v)
            for s, v in b.r.items():
                if skip_self and s == me:
                    continue
                self._wait(e, s, v)

    def _mark(self, semid, val, reads, writes):
        for b in reads:
            b = b.buf if isinstance(b, TT) else b
            if b.r.get(semid, 0) < val:
                b.r[semid] = val
        for b in writes:
            b = b.buf if isinstance(b, TT) else b
            if b.w.get(semid, 0) < val:
                b.w[semid] = val

    def op(self, e, fn, reads=(), writes=(), skip_self=None):
        if skip_self is None:
            skip_self = (e == "pe")
        self._deps(e, reads, writes, skip_self)
        ins = fn(self.eng[e])
        self.cnt[e] += 1
        ins.then_inc(self.sem[e], 1)
        self._mark(id(self.sem[e]), self.cnt[e], reads, writes)
        self.ninst += 1
        return ins

    def dma(self, q, out, in_, reads=(), writes=(), **kw):
        d = self.dq[q]
        K = len(d["sems"])
        i = d["n"] % K
        s = d["sems"][i]
        self._wait(q, id(s), d["tgt"][i])
        self._deps(q, reads, writes)
        ins = self.eng[q].dma_start(out=out, in_=in_, **kw)
        d["tgt"][i] += 16
        d["n"] += 1
        ins.then_inc(s, 16)
        self._mark(id(s), d["tgt"][i], reads, writes)
        self.ninst += 1
        return ins

    def barrier(self):
        toks = []
        for k in self.eng:
            toks.append((id(self.sem[k]), self.cnt[k]))
        for q, d in self.dq.items():
            for s, t in zip(d["sems"], d["tgt"]):
                toks.append((id(s), t))
        for e in self.eng:
            for s, v in toks:
                self._wait(e, s, v)

    def finish(self, e="sp"):
        for k in self.eng:
            self._wait(e, id(self.sem[k]), self.cnt[k])
        for q, d in self.dq.items():
            for s, t in zip(d["sems"], d["tgt"]):
                self._wait(e, id(s), t)


import numpy as np
from contextlib import ExitStack

D = 2048
KC = 16
C_LX, C_LY, C_Q, C_KV, C_G, C_MA, C_MB = 0, 2048, 4096, 6144, 9216, 9264, 11312
DIN = 13360
EPS = 1e-6
NEGBIG = -30000.0


def build(NT, dbg=False, phases=(1, 2, 3, 4)):
    nc = bass.Bass("TRN2", target_bir_lowering=False)
    fw = FW(nc)
    S = NT * 512
    NOWN = NT // 2
    SO = NOWN * 512
    SUP = min(2, NOWN)
    NSUP = NT // SUP
    L = SUP * 512
    skind = "ExternalOutput" if dbg else "Internal"

    def din(name, shape, dt=F32):
        return nc.dram_tensor(name, list(shape), dt, kind="ExternalInput").ap()

    def dscr(name, shape, dt=BF16):
        return TT(nc.dram_tensor(name, list(shape), dt, kind=skind).ap(), name)

    xin = din("xin", [S, D]); pin = din("pin", [SO, 256])
    w_in = din("w_in", [D, DIN]); b_in = din("b_in", [DIN])
    norm_mix_g = din("norm_mix_g", [D])
    conv_w = din("conv_w", [4, D]); conv_b = din("conv_b", [D])
    w_gate_a = din("w_gate_a", [16, 128, 128]); b_gate_a = din("b_gate_a", [16, 128])
    w_gate_x = din("w_gate_x", [16, 128, 128]); b_gate_x = din("b_gate_x", [16, 128])
    lru_lambda = din("lru_lambda", [D])
    w_lru_out = din("w_lru_out", [D, D])
    cmp_pos = [din("cmp_pos_k", [32, 128]), din("cmp_pos_v", [32, 128])]
    cmp_w1 = [din("cmp_w1_k", [4096, 256]), din("cmp_w1_v", [4096, 256])]
    cmp_w2 = [din("cmp_w2_k", [256, 128]), din("cmp_w2_v", [256, 128])]
    w_attn_out = din("w_attn_out", [D, D]); w_o = din("w_o", [D, D])
    norm_mlp_g = din("norm_mlp_g", [D]); w_up = din("w_up", [D, 4 * D]); w_down = din("w_down", [4 * D, D])
    norm_ple_g = din("norm_ple_g", [D]); w_ple_gate = din("w_ple_gate", [D, D]); w_ple_proj = din("w_ple_proj", [256, D])
    norm_final_g = din("norm_final_g", [D])
    c_ident = din("c_ident", [128, 128]); c_flag = din("c_flag", [128, 1]); c_padbias = din("c_padbias", [128, 1])
    c_fc = din("c_fc", [128, 128]); c_gext = din("c_gext", [128, 256]); c_ov = din("c_ov", [512, 129])
    c_expand = din("c_expand", [128, 64 * 128]); c_gsel = din("c_gsel", [48, 48 * 128]); c_cmpbias = din("c_cmpbias", [128, 4])
    out = nc.dram_tensor("out", [SO, D], F32, kind="ExternalOutput").ap()
    outbuf = Buf("out")

    QT = dscr("QT", [16, 128, SO])
    KcmpT = dscr("KcmpT", [4, 128, S]); VcmpT = dscr("VcmpT", [4, 128, S])
    KslcT = dscr("KslcT", [4, 128, S]); KwinT = dscr("KwinT", [4, 128, S])
    Vslc = dscr("Vslc", [S, 512]); Vwin = dscr("Vwin", [S, 512])
    GT = dscr("GT", [48, SO], F32)
    MAT = dscr("MAT", [16, 128, SO]); MBT = dscr("MBT", [16, 128, SO])
    ZaT = dscr("ZaT", [16, 128, SO]); OT = dscr("OT", [16, 128, SO])


    Wb = {"lru": dscr("Wb_lru", [D, D]), "att": dscr("Wb_att", [D, D]), "o": dscr("Wb_o", [D, D]), "up": dscr("Wb_up", [D, 4 * D]),
          "dn": dscr("Wb_dn", [4 * D, D]), "pg": dscr("Wb_pg", [D, D]), "pp": dscr("Wb_pp", [256, D])}
    conv_list = []
    for nm, src in (("lru", w_lru_out), ("att", w_attn_out), ("o", w_o), ("up", w_up), ("dn", w_down), ("pg", w_ple_gate), ("pp", w_ple_proj)):
        R, Cc = src.shape
        rb = max(1, (1 << 20) // Cc)
        for r0 in range(0, R, rb):
            conv_list.append((nm, src, r0, min(rb, R - r0)))
    conv_pos = [0]

    def conv_step(n):
        for _ in range(n):
            if conv_pos[0] >= len(conv_list):
                return
            nm, src, r0, nr = conv_list[conv_pos[0]]
            conv_pos[0] += 1
            fw.dma("pool", Wb[nm][r0:r0 + nr, :], src[r0:r0 + nr, :], writes=[Wb[nm]])

    def sb(name, shape, dt=F32):
        return TT(nc.alloc_sbuf_tensor(name, list(shape), dt), name)

    PS = [TT(nc.alloc_psum_tensor("ps%d" % i, [128, 512], F32), "ps%d" % i) for i in range(8)]
    psi = [0]

    def ps_next():
        p = PS[psi[0] % 8]
        psi[0] += 1
        return p

    identf = sb("identf", [128, 128]); identb = sb("identb", [128, 128], BF16)
    onesb = sb("onesb", [128, 128], BF16)
    flag = sb("flag", [128, 1]); padbias = sb("padbias", [128, 1])
    fw.dma("sp", identf[:, :], c_ident[:, :], writes=[identf])
    fw.dma("pool", identb[:, :], c_ident[:, :], writes=[identb])
    fw.dma("sp", flag[:, :], c_flag[:, :], writes=[flag])
    fw.dma("sp", padbias[:, :], c_padbias[:, :], writes=[padbias])
    fw.op("dve", lambda e: e.memset(onesb[:, :], 1.0), writes=[onesb])
    epsc = sb("epsc", [128, 1]); onec = sb("onec", [128, 1])
    fw.op("dve", lambda e: e.memset(epsc[:, :], EPS), writes=[epsc])
    fw.op("dve", lambda e: e.memset(onec[:, :], 1.0), writes=[onec])

    rr = {"act_dve": 0}

    def evac_engine():
        rr["act_dve"] += 1
        return "act" if rr["act_dve"] % 2 else "dve"

    def copy_op(e, out_ap, in_ap, reads, writes):
        if e == "act":
            fw.op("act", lambda g: g.copy(out=out_ap, in_=in_ap), reads=reads, writes=writes)
        else:
            fw.op(e, lambda g: g.tensor_copy(out=out_ap, in_=in_ap), reads=reads, writes=writes)

    def norm_transpose(src_ap, gb, hT, col0, xtok, xs, ssq, rstd, keep_x32T=None):
        for tb in range(4):
            fw.dma("sp", xtok[tb][:, :], src_ap[tb * 128:(tb + 1) * 128, :], writes=[xtok[tb]])
            fw.op("act", lambda e: e.activation(out=xs[tb][:, :], in_=xtok[tb][:, :], func=AF.Square,
                                                accum_out=ssq[tb][:, 0:1]),
                  reads=[xtok[tb]], writes=[xs[tb], ssq[tb]])
            fw.op("act", lambda e: e.activation(out=rstd[tb][:, :], in_=ssq[tb][:, :], func=AF.Sqrt, scale=1.0 / D, bias=epsc[:, 0:1]),
                  reads=[ssq[tb], epsc], writes=[rstd[tb]])
            fw.op("dve", lambda e: e.reciprocal(out=rstd[tb][:, :], in_=rstd[tb][:, :]), reads=[rstd[tb]], writes=[rstd[tb]])
            fw.op("dve", lambda e: e.scalar_tensor_tensor(out=xs[tb][:, :], in0=xtok[tb][:, :], scalar=rstd[tb][:, 0:1],
                                                          in1=gb[:, :], op0=ALU.mult, op1=ALU.mult),
                  reads=[xtok[tb], rstd[tb], gb], writes=[xs[tb]])
        for i in range(8):
            p = ps_next()
            pb = p.t[:, :].bitcast(BF16)
            for cc in range(2):
                c = 2 * i + cc
                for tb in range(4):
                    fw.op("pe", lambda e: e.transpose(pb[:, cc * 512 + tb * 128: cc * 512 + tb * 128 + 128],
                                                      xs[tb][:, c * 128:(c + 1) * 128], identb[:, :]),
                          reads=[xs[tb], identb], writes=[p])
            copy_op(evac_engine(), hT[:, 2 * i:2 * i + 2, col0:col0 + 512],
                    pb.rearrange("p (a b) -> p a b", a=2), [p], [hT])

    if 1 in phases:
        with ExitStack() as es:
            def sbp(name, shape, dt=F32):
                return TT(es.enter_context(nc.sbuf_tensor(name, list(shape), dt)), name)
            gb = sbp("gb", [128, D])
            fw.dma("sp", gb[:, :], norm_mix_g.rearrange("(o d) -> o d", o=1).partition_broadcast(128), writes=[gb])
            xtok = [sbp("xtok%d" % i, [128, D]) for i in range(4)]
            xs = [sbp("xs%d" % i, [128, D], BF16) for i in range(4)]
            ssq = [sbp("ssq%d" % i, [128, 1]) for i in range(4)]
            rstd = [sbp("rstd%d" % i, [128, 1]) for i in range(4)]
            hT = sbp("hT", [128, KC, L], BF16)
            wslab = [sbp("wslab%d" % i, [128, KC, 512], BF16) for i in range(2)]
            wsi = [0]
            bcol = sbp("bcol", [128, 128])
            convw = sbp("convw", [128, 4, 16]); convb = sbp("convb", [128, 16])
            bga = sbp("bga", [128, 16]); bgx = sbp("bgx", [128, 16]); lam = sbp("lam", [128, 16])
            cs1 = sbp("cs1", [128, 16]); cs2 = sbp("cs2", [128, 16])
            wga = sbp("wga", [128, 16, 128], BF16); wgx = sbp("wgx", [128, 16, 128], BF16)
            vbias = [sbp("vbias%d" % i, [128, 512]) for i in range(2)]
            state = sbp("state", [128, 16]); carry = sbp("carry", [128, 16, 3])
            bqs = sbp("bqs", [128, 16])
            chunks = []
            def addch(c0, n):
                for i in range(n):
                    chunks.append(c0 + 128 * i)
            sec_starts = [(C_LX, 16), (C_LY, 16), (C_Q, 16), (C_KV, 24), (C_MA, 16), (C_MB, 16)]
            bidx = {}
            k = 0
            for c0, n in sec_starts:
                src = b_in[c0:c0 + 128 * n].rearrange("(c p) -> p c", p=128)
                fw.dma("sp", bcol[:, k:k + n], src, writes=[bcol], allow_slow_non_contiguous=True)
                for i in range(n):
                    bidx[c0 + 128 * i] = k + i
                k += n
            fw.dma("sp", bcol[0:48, k:k + 1], b_in[C_G:C_G + 48].rearrange("(p o) -> p o", o=1), writes=[bcol])
            bidx[C_G] = k
            fw.dma("sp", convw[:, :, :], conv_w.rearrange("k (c p) -> p k c", p=128), writes=[convw], allow_slow_non_contiguous=True)
            fw.dma("sp", convb[:, :], conv_b.rearrange("(c p) -> p c", p=128), writes=[convb], allow_slow_non_contiguous=True)
            fw.dma("sp", bga[:, :], b_gate_a.rearrange("c p -> p c"), writes=[bga], allow_slow_non_contiguous=True)
            fw.dma("sp", bgx[:, :], b_gate_x.rearrange("c p -> p c"), writes=[bgx], allow_slow_non_contiguous=True)
            fw.dma("sp", lam[:, :], lru_lambda.rearrange("(c p) -> p c", p=128), writes=[lam], allow_slow_non_contiguous=True)
            fw.dma("pool", wga[:, :, :], w_gate_a.rearrange("n c d -> c n d"), writes=[wga])
            fw.dma("pool", wgx[:, :, :], w_gate_x.rearrange("n c d -> c n d"), writes=[wgx])
            fw.dma("sp", vbias[0][:, :], b_in[C_KV + 3 * 512:C_KV + 4 * 512].rearrange("(o d) -> o d", o=1).partition_broadcast(128), writes=[vbias[0]])
            fw.dma("sp", vbias[1][:, :], b_in[C_KV + 5 * 512:C_KV + 6 * 512].rearrange("(o d) -> o d", o=1).partition_broadcast(128), writes=[vbias[1]])
            fw.op("act", lambda e: e.activation(out=cs1[:, :], in_=lam[:, :], func=AF.Exp, scale=-1.0), reads=[lam], writes=[cs1])
            fw.op("act", lambda e: e.activation(out=cs1[:, :], in_=cs1[:, :], func=AF.Ln, bias=onec[:, 0:1]), reads=[cs1, onec], writes=[cs1])
            fw.op("dve", lambda e: e.tensor_scalar(out=cs2[:, :], in0=cs1[:, :], scalar1=-16.0, scalar2=None, op0=ALU.mult), reads=[cs1], writes=[cs2])
            fw.op("dve", lambda e: e.tensor_scalar(out=cs1[:, :], in0=cs1[:, :], scalar1=-8.0, scalar2=None, op0=ALU.mult), reads=[cs1], writes=[cs1])
            fw.op("dve", lambda e: e.tensor_scalar(out=bqs[:, :], in0=bcol[:, bidx[C_Q]:bidx[C_Q] + 16], scalar1=128.0 ** -0.5, scalar2=None, op0=ALU.mult), reads=[bcol], writes=[bqs])
            fw.op("dve", lambda e: e.memset(state[:, :], 0.0), writes=[state])
            fw.op("dve", lambda e: e.memset(carry[:, :, :], 0.0), writes=[carry])
            ulx = sbp("ulx", [128, 3 + L]); xc = sbp("xc", [128, L]); xcb = sbp("xcb", [128, L], BF16)
            ra = sbp("ra", [128, L]); aa = sbp("aa", [128, L]); ri = sbp("ri", [128, L]); gly = sbp("gly", [128, L])
            zb = sbp("zb", [128, L], BF16)
            otile = [sbp("otile%d" % i, [128, 512], BF16) for i in range(4)]
            oti = [0]
            gtile = [sbp("gtile%d" % i, [48, 512]) for i in range(2)]
            vtile = [sbp("vtile%d" % i, [128, 512], BF16) for i in range(2)]

            def load_slab(c0, ncols):
                w = wslab[wsi[0] % 2]
                wsi[0] += 1
                conv_step(2)
                fw.dma("pool", w[:, :, 0:ncols], w_in[:, c0:c0 + ncols].rearrange("(kc p) n -> p kc n", p=128), writes=[w])
                return w

            def proj_fm(w, cc, ncol, j):
                p = ps_next()
                for kc in range(KC):
                    fw.op("pe", lambda e: e.matmul(p[0:ncol, :], w[:, kc, cc * 128:cc * 128 + ncol], hT[:, kc, j * 512:(j + 1) * 512],
                                                   start=(kc == 0), stop=(kc == KC - 1)), reads=[w, hT], writes=[p])
                return p

            def next_otile():
                t = otile[oti[0] % 4]
                oti[0] += 1
                return t

            for s in range(NSUP):
                own = s >= NSUP // 2
                tok0 = s * L
                own0 = tok0 - SO
                for j in range(SUP):
                    norm_transpose(xin[tok0 + j * 512: tok0 + (j + 1) * 512, :], gb, hT, j * 512, xtok, xs, ssq, rstd)
                for sl in range(4):
                    wlx = load_slab(C_LX + 512 * sl, 512)
                    wly = load_slab(C_LY + 512 * sl, 512) if own else None
                    for cc in range(4):
                        n = 4 * sl + cc
                        fw.op("dve", lambda e: e.tensor_copy(out=ulx[:, 0:3], in_=carry[:, n, :]), reads=[carry], writes=[ulx])
                        for j in range(SUP):
                            p = proj_fm(wlx, cc, 128, j)
                            fw.op("act", lambda e: e.activation(out=ulx[:, 3 + j * 512:3 + (j + 1) * 512], in_=p[:, :], func=AF.Identity,
                                                                bias=bcol[:, bidx[C_LX] + n:bidx[C_LX] + n + 1]), reads=[p, bcol], writes=[ulx])
                            if own:
                                p2 = proj_fm(wly, cc, 128, j)
                                fw.op("act", lambda e: e.activation(out=gly[:, j * 512:(j + 1) * 512], in_=p2[:, :], func=AF.Gelu,
                                                                    bias=bcol[:, bidx[C_LY] + n:bidx[C_LY] + n + 1]), reads=[p2, bcol], writes=[gly])
                        if s == NSUP // 2 - 1:
                            fw.op("dve", lambda e: e.tensor_scalar(out=carry[:, n, :], in0=ulx[:, L:L + 3], scalar1=flag[:, 0:1], scalar2=None, op0=ALU.mult),
                                  reads=[ulx, flag], writes=[carry])
                        else:
                            fw.op("dve", lambda e: e.tensor_copy(out=carry[:, n, :], in_=ulx[:, L:L + 3]), reads=[ulx], writes=[carry])
                        fw.op("dve", lambda e: e.tensor_scalar(out=xc[:, :], in0=ulx[:, 3:3 + L], scalar1=convw[:, 3, n:n + 1], scalar2=convb[:, n:n + 1],
                                                               op0=ALU.mult, op1=ALU.add), reads=[ulx, convw, convb], writes=[xc])
                        for k in range(3):
                            fw.op("dve", lambda e: e.scalar_tensor_tensor(out=xc[:, :], in0=ulx[:, k:k + L], scalar=convw[:, k, n:n + 1], in1=xc[:, :],
                                                                          op0=ALU.mult, op1=ALU.add), reads=[ulx, convw, xc], writes=[xc])
                        fw.op("act", lambda e: e.copy(out=xcb[:, :], in_=xc[:, :]), reads=[xc], writes=[xcb])
                        for j in range(SUP):
                            p = ps_next()
                            fw.op("pe", lambda e: e.matmul(p[:, :], wga[:, n, :], xcb[:, j * 512:(j + 1) * 512], start=True, stop=True), reads=[wga, xcb], writes=[p])
                            fw.op("act", lambda e: e.activation(out=ra[:, j * 512:(j + 1) * 512], in_=p[:, :], func=AF.Sigmoid, bias=bga[:, n:n + 1]),
                                  reads=[p, bga], writes=[ra])
                            p = ps_next()
                            fw.op("pe", lambda e: e.matmul(p[:, :], wgx[:, n, :], xcb[:, j * 512:(j + 1) * 512], start=True, stop=True), reads=[wgx, xcb], writes=[p])
                            fw.op("act", lambda e: e.activation(out=ri[:, j * 512:(j + 1) * 512], in_=p[:, :], func=AF.Sigmoid, bias=bgx[:, n:n + 1]),
                                  reads=[p, bgx], writes=[ri])
                        fw.op("act", lambda e: e.activation(out=aa[:, :], in_=ra[:, :], func=AF.Exp, scale=cs1[:, n:n + 1]), reads=[ra, cs1], writes=[aa])
                        fw.op("act", lambda e: e.activation(out=ra[:, :], in_=ra[:, :], func=AF.Exp, scale=cs2[:, n:n + 1]), reads=[ra, cs2], writes=[ra])
                        fw.op("act", lambda e: e.activation(out=ra[:, :], in_=ra[:, :], func=AF.Sqrt, scale=-1.0, bias=onec[:, 0:1]), reads=[ra, onec], writes=[ra])
                        fw.op("dve", lambda e: e.tensor_tensor(out=ri[:, :], in0=ri[:, :], in1=xc[:, :], op=ALU.mult), reads=[ri, xc], writes=[ri])
                        fw.op("dve", lambda e: e.tensor_tensor(out=ri[:, :], in0=ri[:, :], in1=ra[:, :], op=ALU.mult), reads=[ri, ra], writes=[ri])
                        fw.op("dve", lambda e: e.tensor_tensor_scan(out=xc[:, :], data0=aa[:, :], data1=ri[:, :], initial=state[:, n:n + 1],
                                                                    op0=ALU.mult, op1=ALU.add), reads=[aa, ri, state], writes=[xc])
                        if s == NSUP // 2 - 1:
                            fw.op("dve", lambda e: e.tensor_scalar(out=state[:, n:n + 1], in0=xc[:, L - 1:L], scalar1=flag[:, 0:1], scalar2=None, op0=ALU.mult),
                                  reads=[xc, flag], writes=[state])
                        else:
                            fw.op("dve", lambda e: e.tensor_copy(out=state[:, n:n + 1], in_=xc[:, L - 1:L]), reads=[xc], writes=[state])
                        if own:
                            fw.op("dve", lambda e: e.tensor_tensor(out=zb[:, :], in0=gly[:, :], in1=xc[:, :], op=ALU.mult), reads=[gly, xc], writes=[zb])
                            fw.dma("sp", ZaT[n, :, own0:own0 + L], zb[:, :], reads=[zb], writes=[ZaT])
                for (sec, dst) in ((0, KcmpT), (1, VcmpT), (2, KslcT), (4, KwinT)):
                    w = load_slab(C_KV + 512 * sec, 512)
                    for cc in range(4):
                        for j in range(SUP):
                            p = proj_fm(w, cc, 128, j)
                            t = next_otile()
                            bi = bidx[C_KV] + 4 * sec + cc
                            fw.op("act", lambda e: e.activation(out=t[:, :], in_=p[:, :], func=AF.Identity, bias=bcol[:, bi:bi + 1]), reads=[p, bcol], writes=[t])
                            fw.dma("sp", dst[cc, :, tok0 + j * 512: tok0 + (j + 1) * 512], t[:, :], reads=[t], writes=[dst])
                for (vi, sec, dst) in ((0, 3, Vslc), (1, 5, Vwin)):
                    w = load_slab(C_KV + 512 * sec, 512)
                    for tb in range(SUP * 4):
                        p = ps_next()
                        for kc in range(KC):
                            fw.op("pe", lambda e: e.matmul(p[:, :], hT[:, kc, tb * 128:(tb + 1) * 128], w[:, kc, :], start=(kc == 0), stop=(kc == KC - 1)),
                                  reads=[w, hT], writes=[p])
                        t = vtile[tb % 2]
                        fw.op("dve", lambda e: e.tensor_tensor(out=t[:, :], in0=p[:, :], in1=vbias[vi][:, :], op=ALU.add), reads=[p, vbias[vi]], writes=[t])
                        fw.dma("sp", dst[tok0 + tb * 128: tok0 + (tb + 1) * 128, :], t[:, :], reads=[t], writes=[dst])
                if not own:
                    continue
                for sl in range(4):
                    w = load_slab(C_Q + 512 * sl, 512)
                    for cc in range(4):
                        hd = 4 * sl + cc
                        for j in range(SUP):
                            p = proj_fm(w, cc, 128, j)
                            t = next_otile()
                            fw.op("act", lambda e: e.activation(out=t[:, :], in_=p[:, :], func=AF.Identity, bias=bqs[:, hd:hd + 1], scale=128.0 ** -0.5),
                                  reads=[p, bqs], writes=[t])
                            fw.dma("sp", QT[hd, :, own0 + j * 512: own0 + (j + 1) * 512], t[:, :], reads=[t], writes=[QT])
                w = load_slab(C_G, 48)
                for j in range(SUP):
                    p = proj_fm(w, 0, 48, j)
                    t = gtile[j % 2]
                    fw.op("act", lambda e: e.activation(out=t[:, :], in_=p[0:48, :], func=AF.Sigmoid, bias=bcol[0:48, bidx[C_G]:bidx[C_G] + 1]),
                          reads=[p, bcol], writes=[t])
                    fw.dma("sp", GT[:, own0 + j * 512: own0 + (j + 1) * 512], t[:, :], reads=[t], writes=[GT])
                for (c0, dst) in ((C_MA, MAT), (C_MB, MBT)):
                    for sl in range(4):
                        w = load_slab(c0 + 512 * sl, 512)
                        for cc in range(4):
                            n = 4 * sl + cc
                            for j in range(SUP):
                                p = proj_fm(w, cc, 128, j)
                                t = next_otile()
                                fw.op("act", lambda e: e.activation(out=t[:, :], in_=p[:, :], func=AF.Sigmoid, bias=bcol[:, bidx[c0] + n:bidx[c0] + n + 1]),
                                      reads=[p, bcol], writes=[t])
                                fw.dma("sp", dst[n, :, own0 + j * 512: own0 + (j + 1) * 512], t[:, :], reads=[t], writes=[dst])
            fw.barrier()


    NCB = S // 16 - 1
    NBC = (NCB + 127) // 128
    kcT = [sb("kcT%d" % g, [128, 512], BF16) for g in range(4)]
    vc = [sb("vc%d" % g, [128, 4, 128], BF16) for g in range(4)]
    ACC = PS[4:8]
    wk = [0]

    def ps_work():
        p = PS[wk[0] % 4]
        wk[0] += 1
        return p

    if 2 in phases:
        with ExitStack() as es:
            def sbp(name, shape, dt=F32):
                return TT(es.enter_context(nc.sbuf_tensor(name, list(shape), dt)), name)
            w1s = sbp("w1s", [128, 32, 256], BF16); w2s = sbp("w2s", [128, 2, 128], BF16)
            pos32 = sbp("pos32", [32, 128]); posT = sbp("posT", [128, 32], BF16); cb = sbp("cb", [128, 2])
            KT = [sbp("KTc%d" % i, [128, S], BF16) for i in range(2)]
            hid = [sbp("hid%d" % i, [128, 512], BF16) for i in range(2)]
            for g in range(4):
                fw.op("dve", lambda e: e.memset(kcT[g][:, :], 0.0), writes=[kcT[g]])
                fw.op("dve", lambda e: e.memset(vc[g][:, :, :], 0.0), writes=[vc[g]])
            fw.op("dve", lambda e: e.memset(hid[0][:, :], 0.0), writes=[hid[0]])
            fw.op("dve", lambda e: e.memset(hid[1][:, :], 0.0), writes=[hid[1]])
            for kv in range(2):
                for i in range(4):
                    fw.dma("pool", w1s[:, 8 * i:8 * i + 8, :], cmp_w1[kv][1024 * i:1024 * (i + 1), :].rearrange("(pos d) h -> d pos h", d=128), writes=[w1s])
                fw.dma("pool", w2s[:, :, :], cmp_w2[kv].rearrange("(hc p) d -> p hc d", p=128), writes=[w2s])
                fw.dma("sp", pos32[:, :], cmp_pos[kv][:, :], writes=[pos32])
                p = ps_work()
                fw.op("pe", lambda e: e.transpose(p[:, 0:32], pos32[:, :], identf[0:32, 0:32]), reads=[pos32, identf], writes=[p])
                fw.op("dve", lambda e: e.tensor_copy(out=posT[:, :], in_=p[:, 0:32]), reads=[p], writes=[posT])
                for hc in range(2):
                    p = ps_work()
                    for pos in range(32):
                        fw.op("pe", lambda e: e.matmul(p[:, 0:1], w1s[:, pos, hc * 128:(hc + 1) * 128], posT[:, pos:pos + 1], start=(pos == 0), stop=(pos == 31)),
                              reads=[w1s, posT], writes=[p])
                    fw.op("dve", lambda e: e.tensor_copy(out=cb[:, hc:hc + 1], in_=p[:, 0:1]), reads=[p], writes=[cb])
                src = KcmpT if kv == 0 else VcmpT
                for g in range(4):
                    kt = KT[g % 2]
                    fw.dma("sp", kt[:, :], src[g, :, :], reads=[src], writes=[kt])
                    for hc in range(2):
                        p = ps_work()
                        for pos in range(32):
                            fw.op("pe", lambda e: e.matmul(p[:, 0:NCB], w1s[:, pos, hc * 128:(hc + 1) * 128], kt[:, pos:pos + 16 * (NCB - 1) + 1:16],
                                                           start=(pos == 0), stop=(pos == 31)), reads=[w1s, kt], writes=[p])
                        fw.op("act", lambda e: e.activation(out=hid[hc][:, 0:NCB], in_=p[:, 0:NCB], func=AF.Gelu, bias=cb[:, hc:hc + 1]), reads=[p, cb], writes=[hid[hc]])
                    if kv == 0:
                        p = ps_work()
                        for hc in range(2):
                            fw.op("pe", lambda e: e.matmul(p[:, 0:NCB], w2s[:, hc, :], hid[hc][:, 0:NCB], start=(hc == 0), stop=(hc == 1)), reads=[w2s, hid[hc]], writes=[p])
                        fw.op("dve", lambda e: e.tensor_copy(out=kcT[g][:, 0:NCB], in_=p[:, 0:NCB]), reads=[p], writes=[kcT[g]])
                    else:
                        for bc in range(NBC):
                            nb = min(128, NCB - bc * 128)
                            p = ps_work()
                            for hc in range(2):
                                fw.op("pe", lambda e: e.matmul(p[0:nb, 0:128], hid[hc][:, bc * 128:bc * 128 + nb], w2s[:, hc, :], start=(hc == 0), stop=(hc == 1)),
                                      reads=[w2s, hid[hc]], writes=[p])
                            fw.op("dve", lambda e: e.tensor_copy(out=vc[g][0:nb, bc, :], in_=p[0:nb, 0:128]), reads=[p], writes=[vc[g]])
            fw.barrier()

    if 3 in phases:
        with ExitStack() as es:
            def sbp(name, shape, dt=F32):
                return TT(es.enter_context(nc.sbuf_tensor(name, list(shape), dt)), name)
            NKC = S // 128
            ovx = sbp("ovx", [128, 4, 129], BF16); fc = sbp("fc", [128, 128]); gext = sbp("gext", [128, 256])
            expand = sbp("expand", [128, 64, 128], BF16); gsel = sbp("gsel", [48, 48, 128], BF16)
            cmpbias = sbp("cmpbias", [128, 4])
            fw.dma("pool", ovx[:, :, :], c_ov.rearrange("(c p) n -> p c n", p=128), writes=[ovx])
            fw.dma("sp", fc[:, :], c_fc[:, :], writes=[fc]); fw.dma("sp", gext[:, :], c_gext[:, :], writes=[gext])
            for i in range(4):
                fw.dma("pool", expand[:, 16 * i:16 * i + 16, :], c_expand[:, 2048 * i:2048 * (i + 1)].rearrange("p (a b) -> p a b", b=128), writes=[expand])
            fw.dma("pool", gsel[:, :, :], c_gsel.rearrange("p (a b) -> p a b", b=128), writes=[gsel])
            fw.dma("sp", cmpbias[:, :], c_cmpbias[:, :], writes=[cmpbias])
            KsT = sbp("KsT", [128, S], BF16); KwT = sbp("KwT", [128, S], BF16)
            Vs = sbp("Vs", [128, NKC, 128], BF16); Vw = sbp("Vw", [128, NKC, 128], BF16)
            qT = [sbp("qT%d" % i, [128, 4, 512], BF16) for i in range(2)]
            g32 = [sbp("g32_%d" % i, [48, 512]) for i in range(2)]; gTb = [sbp("gTb%d" % i, [48, 512], BF16) for i in range(2)]
            pT = [sbp("pT%d" % i, [128, 512], BF16) for i in range(8)]
            sacc = [sbp("sacc%d" % i, [128, 512]) for i in range(4)]
            onesf = sbp("onesf", [128, 128])
            fw.op("dve", lambda e: e.memset(onesf[:, :], 1.0), writes=[onesf])
            pti = [0]
            oacc = [sbp("oacc%d" % i, [128, 512]) for i in range(4)]
            otmp = [sbp("otmp%d" % i, [128, 512]) for i in range(2)]
            rs = [sbp("rs%d" % i, [128, 512]) for i in range(2)]
            wgt = [sbp("wgt%d" % i, [128, 512]) for i in range(2)]
            impacc = [sbp("impacc%d" % i, [128, 128]) for i in range(4)]
            rec = [sbp("rec%d" % i, [128, 1]) for i in range(4)]
            score = [sbp("score%d" % i, [128, 128]) for i in range(2)]
            scw = [sbp("scw%d" % i, [128, 128]) for i in range(2)]
            m8 = [sbp("m8_%d" % i, [128, 16]) for i in range(2)]
            maskq = [sbp("maskq%d" % i, [128, 128], BF16) for i in range(2)]
            maskT = sbp("maskT", [128, 512], BF16)
            obf = [sbp("obf%d" % i, [128, 512], BF16) for i in range(2)]
            cnt = {"fin": 0, "ob": 0, "sum": 0, "set": 0}

            def next_pT():
                t = pT[pti[0] % 8]
                pti[0] += 1
                return t

            def finalize(h, br, o_ps, sa, gt, first):
                i = cnt["fin"] % 2
                cnt["fin"] += 1
                s_ps = ps_work()
                fw.op("pe", lambda e: e.matmul(s_ps[:, :], onesf[:, :], sa[:, :], start=True, stop=True), reads=[onesf, sa], writes=[s_ps])
                fw.op("dve", lambda e: e.tensor_scalar(out=rs[i][:, :], in0=s_ps[:, :], scalar1=1e-30, scalar2=None, op0=ALU.max), reads=[s_ps], writes=[rs[i]])
                fw.op("dve", lambda e: e.reciprocal(out=rs[i][:, :], in_=rs[i][:, :]), reads=[rs[i]], writes=[rs[i]])
                pg = ps_work()
                fw.op("pe", lambda e: e.matmul(pg[:, :], gsel[:, br * 16 + h, :], gt[:, :], start=True, stop=True), reads=[gsel, gt], writes=[pg])
                fw.op("dve", lambda e: e.tensor_tensor(out=wgt[i][:, :], in0=rs[i][:, :], in1=pg[:, :], op=ALU.mult), reads=[rs[i], pg], writes=[wgt[i]])
                hl_ = h % 4
                if first:
                    fw.op("dve", lambda e: e.tensor_tensor(out=oacc[hl_][:, :], in0=o_ps[:, :], in1=wgt[i][:, :], op=ALU.mult), reads=[o_ps, wgt[i]], writes=[oacc[hl_]])
                else:
                    fw.op("dve", lambda e: e.tensor_tensor(out=otmp[i][:, :], in0=o_ps[:, :], in1=wgt[i][:, :], op=ALU.mult), reads=[o_ps, wgt[i]], writes=[otmp[i]])
                    fw.op("pool", lambda e: e.tensor_tensor(out=oacc[hl_][:, :], in0=oacc[hl_][:, :], in1=otmp[i][:, :], op=ALU.add), reads=[oacc[hl_], otmp[i]], writes=[oacc[hl_]])

            LA = 4

            def pipeline(items):
                n = len(items)
                for i in range(min(LA, n)):
                    items[i][0]()
                for i in range(n):
                    items[i][1]()
                    if i + LA < n:
                        items[i + LA][0]()

            def make_item(score_mms, exp_bias, selects, acc_mms, post=None, sum_acc=None):
                st = {}

                def A():
                    p = ps_work()
                    t = next_pT()
                    st["t"] = t
                    nmm = len(score_mms)
                    for i, (l, r, rd) in enumerate(score_mms):
                        fw.op("pe", lambda e: e.matmul(p[:, :], l, r, start=(i == 0), stop=(i == nmm - 1)), reads=rd, writes=[p])
                    if exp_bias is None:
                        fw.op("act", lambda e: e.activation(out=t[:, :], in_=p[:, :], func=AF.Exp), reads=[p], writes=[t])
                    else:
                        fw.op("act", lambda e: e.activation(out=t[:, :], in_=p[:, :], func=AF.Exp, bias=exp_bias[0]), reads=[p, exp_bias[1]], writes=[t])
                    for (pat, base, cm) in selects:
                        fw.op("pool", lambda e: e.affine_select(out=t[:, :], in_=t[:, :], pattern=pat, compare_op=ALU.is_ge, fill=0.0, base=base, channel_multiplier=cm),
                              reads=[t], writes=[t])

                def B():
                    t = st["t"]
                    for (o_tt, fn, rd) in acc_mms:
                        fw.op("pe", lambda e: fn(e, t), reads=rd + [t], writes=[o_tt])
                    if sum_acc is not None:
                        sa, first_ = sum_acc
                        eng_ = "dve" if cnt["sum"] % 2 == 0 else "pool"
                        cnt["sum"] += 1
                        if first_:
                            fw.op(eng_, lambda e: e.tensor_copy(out=sa[:, :], in_=t[:, :]), reads=[t], writes=[sa])
                        else:
                            fw.op(eng_, lambda e: e.tensor_tensor(out=sa[:, :], in0=sa[:, :], in1=t[:, :], op=ALU.add), reads=[sa, t], writes=[sa])
                    if post is not None:
                        post()
                return (A, B)

            for g in range(4):
                fw.dma("sp", KsT[:, :], KslcT[g, :, :], reads=[KslcT], writes=[KsT])
                fw.dma("sp", KwT[:, :], KwinT[g, :, :], reads=[KwinT], writes=[KwT])
                for i in range(0, NKC, 16):
                    n = min(16, NKC - i)
                    fw.dma("sp", Vs[:, i:i + n, :], Vslc[i * 128:(i + n) * 128, g * 128:(g + 1) * 128].rearrange("(kc p) d -> p kc d", p=128), reads=[Vslc], writes=[Vs])
                    fw.dma("sp", Vw[:, i:i + n, :], Vwin[i * 128:(i + n) * 128, g * 128:(g + 1) * 128].rearrange("(kc p) d -> p kc d", p=128), reads=[Vwin], writes=[Vw])
                for qt in range(NOWN):
                    q0 = SO + qt * 512
                    qq = qT[qt % 2]; gf = g32[qt % 2]; gt = gTb[qt % 2]
                    fw.dma("sp", qq[:, :, :], QT[4 * g:4 * g + 4, :, qt * 512:(qt + 1) * 512].rearrange("h p t -> p h t"), reads=[QT], writes=[qq])
                    fw.dma("sp", gf[:, :], GT[:, qt * 512:(qt + 1) * 512], reads=[GT], writes=[gf])
                    fw.op("act", lambda e: e.copy(out=gt[:, :], in_=gf[:, :]), reads=[gf], writes=[gt])
                    items = []
                    ncc = min(NBC, ((q0 + 480) // 16) // 128 + 1)
                    o_ps, i_ps0, i_ps1 = ACC[0], ACC[2], ACC[3]

                    def cmp_post(h):
                        def f():
                            finalize(4 * g + h, 0, o_ps, sacc[h % 2], gt, True)
                            for qs in range(4):
                                ip = i_ps0 if qs < 2 else i_ps1
                                c0 = (qs % 2) * 129
                                fw.op("dve", lambda e: e.tensor_scalar(out=rec[qs][:, :], in0=ip[:, c0 + 128:c0 + 129], scalar1=1e-30, scalar2=None, op0=ALU.max), reads=[ip], writes=[rec[qs]])
                                fw.op("dve", lambda e: e.reciprocal(out=rec[qs][:, :], in_=rec[qs][:, :]), reads=[rec[qs]], writes=[rec[qs]])
                                if h == 0:
                                    fw.op("dve", lambda e: e.tensor_scalar(out=impacc[qs][:, :], in0=ip[:, c0:c0 + 128], scalar1=rec[qs][:, 0:1], scalar2=None, op0=ALU.mult),
                                          reads=[ip, rec[qs]], writes=[impacc[qs]])
                                else:
                                    fw.op("dve", lambda e: e.scalar_tensor_tensor(out=impacc[qs][:, :], in0=ip[:, c0:c0 + 128], scalar=rec[qs][:, 0:1], in1=impacc[qs][:, :],
                                                                                  op0=ALU.mult, op1=ALU.add), reads=[ip, rec[qs], impacc[qs]], writes=[impacc[qs]])
                        return f

                    for h in range(4):
                        for cbk in range(ncc):
                            sel = []
                            if (cbk * 128 + 127) * 16 + 31 > q0:
                                sel.append(([[1, 512]], q0 - cbk * 2048 - 31, -16))
                            st_, sp_ = (cbk == 0), (cbk == ncc - 1)
                            accm = [(o_ps, (lambda e, t, cbk=cbk, st_=st_, sp_=sp_: e.matmul(o_ps[:, :], vc[g][:, cbk, :], t[:, :], start=st_, stop=sp_)), [vc[g]])]
                            for qs in range(4):
                                ip = i_ps0 if qs < 2 else i_ps1
                                accm.append((ip, (lambda e, t, ip=ip, qs=qs, cbk=cbk, st_=st_, sp_=sp_: e.matmul(ip[:, (qs % 2) * 129:(qs % 2) * 129 + 129], t[:, qs * 128:(qs + 1) * 128],
                                                                                                         ovx[:, cbk, :], start=st_, stop=sp_)), [ovx]))
                            items.append(make_item([(kcT[g][:, cbk * 128:(cbk + 1) * 128], qq[:, h, :], [kcT[g], qq])], (cmpbias[:, cbk:cbk + 1], cmpbias), sel, accm,
                                                   cmp_post(h) if cbk == ncc - 1 else None, sum_acc=(sacc[h % 2], st_)))
                    pipeline(items)
                    pm = ps_work()
                    pmb = pm.t[:, :].bitcast(BF16)
                    for qs in range(4):
                        i = qs % 2
                        off = 128 - (q0 + qs * 128) // 64
                        fw.op("dve", lambda e: e.tensor_tensor(out=score[i][:, :], in0=impacc[qs][:, :], in1=fc[:, :], op=ALU.add), reads=[impacc[qs], fc], writes=[score[i]])
                        fw.op("dve", lambda e: e.tensor_tensor(out=score[i][:, :], in0=score[i][:, :], in1=gext[:, off:off + 128], op=ALU.add), reads=[score[i], gext], writes=[score[i]])
                        fw.op("dve", lambda e: e.max(out=m8[i][:, 0:8], in_=score[i][:, :]), reads=[score[i]], writes=[m8[i]])
                        fw.op("dve", lambda e: e.match_replace(out=scw[i][:, :], in_to_replace=m8[i][:, 0:8], in_values=score[i][:, :], imm_value=-1e30),
                              reads=[score[i], m8[i]], writes=[scw[i]])
                        fw.op("dve", lambda e: e.max(out=m8[i][:, 8:16], in_=scw[i][:, :]), reads=[scw[i]], writes=[m8[i]])
                        fw.op("dve", lambda e: e.tensor_scalar(out=maskq[i][:, :], in0=score[i][:, :], scalar1=m8[i][:, 15:16], scalar2=NEGBIG, op0=ALU.is_lt, op1=ALU.mult),
                              reads=[score[i], m8[i]], writes=[maskq[i]])
                        fw.op("pe", lambda e: e.transpose(pmb[:, qs * 128:(qs + 1) * 128], maskq[i][:, :], identb[:, :]), reads=[maskq[i], identb], writes=[pm])
                    fw.op("act", lambda e: e.copy(out=maskT[:, :], in_=pmb[:, 0:512]), reads=[pm], writes=[maskT])
                    nks = (q0 + 512) // 128
                    kc0 = (q0 - 512) // 128
                    for hp in range(2):
                        hs = (2 * hp, 2 * hp + 1)
                        items = []
                        for br in (1, 2):
                            sset = cnt["set"] % 2
                            cnt["set"] += 1
                            accs = {hs[0]: (ACC[2 * sset], sacc[2 * sset]), hs[1]: (ACC[2 * sset + 1], sacc[2 * sset + 1])}
                            kfirst = 0 if br == 1 else kc0
                            for kc in range(kfirst, nks):
                                for h in hs:
                                    o_a, s_a = accs[h]
                                    st_, sp_ = (kc == kfirst), (kc == nks - 1)
                                    if br == 1:
                                        sel = [([[1, 512]], q0 - kc * 128, -1)] if kc * 128 >= q0 else []
                                        vv = Vs
                                        smm = [(expand[:, kc, :], maskT[:, :], [expand, maskT]), (KsT[:, kc * 128:(kc + 1) * 128], qq[:, h, :], [KsT, qq])]
                                    else:
                                        o = kc * 128 - q0
                                        sel = [([[-1, 512]], 511 + o, 1)] if o < 0 else [([[1, 512]], -o, -1)]
                                        vv = Vw
                                        smm = [(KwT[:, kc * 128:(kc + 1) * 128], qq[:, h, :], [KwT, qq])]
                                    accm = [(o_a, (lambda e, t, o_a=o_a, kc=kc, st_=st_, sp_=sp_, vv=vv: e.matmul(o_a[:, :], vv[:, kc, :], t[:, :], start=st_, stop=sp_)), [vv])]
                                    post = None
                                    if kc == nks - 1:
                                        post = (lambda h=h, o_a=o_a, s_a=s_a, br=br: finalize(4 * g + h, br, o_a, s_a, gt, False))
                                    items.append(make_item(smm, (padbias[:, 0:1], padbias) if kc * 128 < SO else None, sel, accm, post, sum_acc=(s_a, st_)))
                        pipeline(items)
                    for h in range(4):
                        ob = obf[cnt["ob"] % 2]
                        cnt["ob"] += 1
                        fw.op("act", lambda e: e.copy(out=ob[:, :], in_=oacc[h][:, :]), reads=[oacc[h]], writes=[ob])
                        fw.dma("sp", OT[4 * g + h, :, qt * 512:(qt + 1) * 512], ob[:, :], reads=[ob], writes=[OT])
            fw.barrier()


    if 4 in phases:
        conv_step(len(conv_list))
        with ExitStack() as es:
            def sbp(name, shape, dt=F32):
                return TT(es.enter_context(nc.sbuf_tensor(name, list(shape), dt)), name)
            A1 = sbp("A1", [128, 16, 512], BF16); A2 = sbp("A2", [128, 16, 512], BF16)
            Mt = [sbp("Mt%d" % i, [128, 4, 512], BF16) for i in range(2)]
            xT = sbp("xT", [128, 16, 512]); U = sbp("U", [128, 32, 512], BF16)
            slabs = [sbp("slab%d" % i, [128, 8192], BF16) for i in range(2)]
            sli = [0]
            xtk = [sbp("xtk%d" % i, [128, D]) for i in range(2)]
            wpp = sbp("wpp", [128, 2, D], BF16); pT = sbp("pT", [128, 2, 512], BF16)
            ptok = [sbp("ptok%d" % i, [128, 256]) for i in range(2)]; ptb = [sbp("ptb%d" % i, [128, 256], BF16) for i in range(2)]
            gcols = sbp("gcols", [128, 3, 16])
            sq = [sbp("sq%d" % i, [128, 512], BF16) for i in range(3)]
            rstdb = sbp("rstdb", [128, 512]); tmpf = [sbp("tmpf%d" % i, [128, 512]) for i in range(2)]
            rl = [sbp("rl%d" % i, [128, 512], BF16) for i in range(2)]
            k4 = {"sq": 0, "tmp": 0, "rl": 0, "q": 0}
            for i, gsrc in enumerate((norm_mlp_g, norm_ple_g, norm_final_g)):
                fw.dma("sp", gcols[:, i, :], gsrc.rearrange("(c p) -> p c", p=128), writes=[gcols], allow_slow_non_contiguous=True)
            fw.dma("sp", wpp[:, :, :], Wb["pp"][:, :].rearrange("(kc p) n -> p kc n", p=128), reads=[Wb["pp"]], writes=[wpp])

            def load_slab4(wb, r0, nk, c0, ncols):
                sl_ = slabs[sli[0] % 2]
                sli[0] += 1
                v = sl_.t[:, 0:nk * ncols].rearrange("p (k n) -> p k n", n=ncols)
                fw.dma("sp", v[:, :, :], wb[r0:r0 + nk * 128, c0:c0 + ncols].rearrange("(kc p) n -> p kc n", p=128), reads=[wb], writes=[sl_])
                return sl_, v

            def linear(wb, r0, nk, ncol_total, act_in, k_off, epilogue, slab_cols=512, col_off=0):
                for c0 in range(0, ncol_total, slab_cols):
                    sl_, v = load_slab4(wb, r0, nk, c0 + col_off, slab_cols)
                    for cc in range(slab_cols // 128):
                        p = ps_work()
                        for kc in range(nk):
                            fw.op("pe", lambda e: e.matmul(p[:, :], v[:, kc, cc * 128:(cc + 1) * 128], act_in[:, k_off + kc, :], start=(kc == 0), stop=(kc == nk - 1)),
                                  reads=[sl_, act_in], writes=[p])
                        epilogue(c0 // 128 + cc, p)

            def rms_to(gi, dst):
                pss = ACC[0]
                for n in range(16):
                    t = sq[k4["sq"] % 3]
                    k4["sq"] += 1
                    fw.op("act", lambda e: e.activation(out=t[:, :], in_=xT[:, n, :], func=AF.Square), reads=[xT], writes=[t])
                    fw.op("pe", lambda e: e.matmul(pss[:, :], onesb[:, :], t[:, :], start=(n == 0), stop=(n == 15)), reads=[onesb, t], writes=[pss])
                fw.op("act", lambda e: e.activation(out=rstdb[:, :], in_=pss[:, :], func=AF.Sqrt, scale=1.0 / D, bias=epsc[:, 0:1]), reads=[pss, epsc], writes=[rstdb])
                fw.op("dve", lambda e: e.reciprocal(out=rstdb[:, :], in_=rstdb[:, :]), reads=[rstdb], writes=[rstdb])
                for n in range(16):
                    fw.op("dve", lambda e: e.scalar_tensor_tensor(out=dst[:, n, :], in0=xT[:, n, :], scalar=gcols[:, gi, n:n + 1], in1=rstdb[:, :],
                                                                  op0=ALU.mult, op1=ALU.mult), reads=[xT, gcols, rstdb], writes=[dst])

            for qt in range(NOWN):
                t0_ = qt * 512
                for (src, wname, gsrc, first) in ((ZaT, "lru", MAT, True), (OT, "att", MBT, False)):
                    fw.dma("sp", A1[:, :, :], src[:, :, t0_:t0_ + 512].rearrange("n p t -> p n t"), reads=[src], writes=[A1])
                    mt_cur = {}

                    def epi(n, p, gsrc=gsrc, first=first, mt_cur=mt_cur):
                        if n % 4 == 0:
                            m = Mt[(n // 4) % 2]
                            fw.dma("sp", m[:, :, :], gsrc[n:n + 4, :, t0_:t0_ + 512].rearrange("n p t -> p n t"), reads=[gsrc], writes=[m])
                            mt_cur["m"] = m
                        m = mt_cur["m"]
                        if first:
                            fw.op("dve", lambda e: e.tensor_tensor(out=A2[:, n, :], in0=p[:, :], in1=m[:, n % 4, :], op=ALU.mult), reads=[p, m], writes=[A2])
                        else:
                            tf = tmpf[k4["tmp"] % 2]
                            k4["tmp"] += 1
                            fw.op("dve", lambda e: e.tensor_tensor(out=tf[:, :], in0=p[:, :], in1=m[:, n % 4, :], op=ALU.mult), reads=[p, m], writes=[tf])
                            fw.op("pool", lambda e: e.tensor_tensor(out=A2[:, n, :], in0=A2[:, n, :], in1=tf[:, :], op=ALU.add), reads=[A2, tf], writes=[A2])
                    linear(Wb[wname], 0, 16, D, A1, 0, epi)
                for tb in range(4):
                    xk = xtk[tb % 2]
                    fw.dma("sp", xk[:, :], xin[SO + t0_ + tb * 128: SO + t0_ + (tb + 1) * 128, :], writes=[xk])
                    for cg in range(4):
                        p = ps_work()
                        for cc in range(4):
                            c = 4 * cg + cc
                            fw.op("pe", lambda e: e.transpose(p[:, cc * 128:(cc + 1) * 128], xk[:, c * 128:(c + 1) * 128], identf[:, :]), reads=[xk, identf], writes=[p])
                        copy_op(evac_engine(), xT[:, 4 * cg:4 * cg + 4, tb * 128:(tb + 1) * 128], p[:, :].rearrange("p (a b) -> p a b", a=4), [p], [xT])
                def epi_add(n, p):
                    fw.op("dve", lambda e: e.tensor_tensor(out=xT[:, n, :], in0=xT[:, n, :], in1=p[:, :], op=ALU.add), reads=[xT, p], writes=[xT])
                linear(Wb["o"], 0, 16, D, A2, 0, epi_add)
                rms_to(0, A1)
                for half in range(2):
                    def epi_up(n, p):
                        r_ = rl[k4["rl"] % 2]
                        k4["rl"] += 1
                        fw.op("act", lambda e: e.activation(out=r_[:, :], in_=p[:, :], func=AF.Relu), reads=[p], writes=[r_])
                        fw.op("pool", lambda e: e.tensor_tensor(out=U[:, n, :], in0=r_[:, :], in1=r_[:, :], op=ALU.mult), reads=[r_], writes=[U])
                    linear(Wb["up"], 0, 16, 4096, A1, 0, epi_up, col_off=half * 4096)
                    linear(Wb["dn"], half * 4096, 32, D, U, 0, epi_add, slab_cols=256)
                rms_to(1, A1)
                for tb in range(4):
                    pk = ptok[tb % 2]; pb_ = ptb[tb % 2]
                    fw.dma("sp", pk[:, :], pin[t0_ + tb * 128:t0_ + (tb + 1) * 128, :], writes=[pk])
                    fw.op("act", lambda e: e.copy(out=pb_[:, :], in_=pk[:, :]), reads=[pk], writes=[pb_])
                    p = ps_work()
                    ppb = p.t[:, :].bitcast(BF16)
                    for pc in range(2):
                        fw.op("pe", lambda e: e.transpose(ppb[:, pc * 128:(pc + 1) * 128], pb_[:, pc * 128:(pc + 1) * 128], identb[:, :]), reads=[pb_, identb], writes=[p])
                    fw.op("dve", lambda e: e.tensor_copy(out=pT[:, :, tb * 128:(tb + 1) * 128], in_=ppb[:, 0:256].rearrange("p (a b) -> p a b", a=2)), reads=[p], writes=[pT])

                def epi_ple(n, p):
                    tf = tmpf[k4["tmp"] % 2]
                    k4["tmp"] += 1
                    fw.op("act", lambda e: e.activation(out=tf[:, :], in_=p[:, :], func=AF.Sigmoid), reads=[p], writes=[tf])
                    p2 = ps_work()
                    for pc in range(2):
                        fw.op("pe", lambda e: e.matmul(p2[:, :], wpp[:, pc, n * 128:(n + 1) * 128], pT[:, pc, :], start=(pc == 0), stop=(pc == 1)), reads=[wpp, pT], writes=[p2])
                    fw.op("dve", lambda e: e.tensor_tensor(out=tf[:, :], in0=tf[:, :], in1=p2[:, :], op=ALU.mult), reads=[tf, p2], writes=[tf])
                    fw.op("pool", lambda e: e.tensor_tensor(out=xT[:, n, :], in0=xT[:, n, :], in1=tf[:, :], op=ALU.add), reads=[xT, tf], writes=[xT])
                linear(Wb["pg"], 0, 16, D, A1, 0, epi_ple)
                rms_to(2, xT)
                for tb in range(4):
                    xk = xtk[tb % 2]
                    for cg in range(4):
                        p = ps_work()
                        for cc in range(4):
                            c = 4 * cg + cc
                            fw.op("pe", lambda e: e.transpose(p[:, cc * 128:(cc + 1) * 128], xT[:, c, tb * 128:(tb + 1) * 128], identf[:, :]), reads=[xT, identf], writes=[p])
                        copy_op(evac_engine(), xk[:, cg * 512:(cg + 1) * 512], p[:, :], [p], [xk])
                    fw.dma("sp", out[t0_ + tb * 128:t0_ + (tb + 1) * 128, :], xk[:, :], reads=[xk], writes=[outbuf])
            fw.barrier()

    fw.finish("sp")
    return nc, fw


def make_consts(c, NT):
    S = NT * 512; SO = S // 2
    NCB = S // 16 - 1
    o = {}
    o["c_ident"] = np.eye(128, dtype=np.float32)
    o["c_flag"] = np.full((128, 1), float(c), np.float32)
    o["c_padbias"] = np.full((128, 1), 0.0 if c else -30000.0, np.float32)
    fc = np.zeros(128, np.float32)
    f0 = 0 if c else SO // 64
    fc[:f0] = -2e4
    fc[f0] = 1e4
    o["c_fc"] = np.ascontiguousarray(np.broadcast_to(fc[None, :], (128, 128)))
    qq = np.arange(128)[:, None]; rp = np.arange(256)[None, :]
    r = rp - 128 - qq // 64
    o["c_gext"] = np.where((r == 0) | (r == -1), 1e4, np.where(r > 0, -1e4, 0.0)).astype(np.float32)
    ov = np.zeros((512, 129), np.float32)
    cb = np.arange(512)[:, None]; j = np.arange(128)[None, :]
    ov[:, :128] = ((16 * cb < 64 * j + 64) & (64 * j < 16 * cb + 32))
    ov[:, 128] = 1.0
    ov[NCB:, :] = 0.0
    o["c_ov"] = ov
    kk = np.arange(64 * 128)[None, :]; jj = np.arange(128)[:, None]
    o["c_expand"] = (kk // 64 == jj).astype(np.float32)
    ii = np.arange(48 * 128)[None, :] // 128; rr = np.arange(48)[:, None]
    o["c_gsel"] = (ii == rr).astype(np.float32)
    blk = np.arange(4)[None, :] * 128 + np.arange(128)[:, None]
    o["c_cmpbias"] = np.where((blk < SO // 16) & (c == 0), -30000.0, 0.0).astype(np.float32)
    return o


_NC_CACHE = {}


def kernel(**inputs):
    NT = 16
    S = NT * 512
    SO = S // 2
    x = np.asarray(inputs["x"], dtype=np.float32)
    p = np.asarray(inputs["p"], dtype=np.float32)
    B = x.shape[0]
    if "nc" not in _NC_CACHE:
        _NC_CACHE["nc"] = build(NT, dbg=False)[0]
    nc = _NC_CACHE["nc"]
    wnames = ["norm_mix_g", "w_in", "b_in", "conv_w", "conv_b", "w_gate_a", "b_gate_a", "w_gate_x", "b_gate_x", "lru_lambda",
              "w_lru_out", "cmp_pos_k", "cmp_w1_k", "cmp_w2_k", "cmp_pos_v", "cmp_w1_v", "cmp_w2_v", "w_attn_out", "w_o",
              "norm_mlp_g", "w_up", "w_down", "norm_ple_g", "w_ple_gate", "w_ple_proj"]
    W = {n: np.ascontiguousarray(np.asarray(inputs[n], dtype=np.float32)[0]) for n in wnames}
    W["norm_final_g"] = np.ascontiguousarray(np.asarray(inputs["norm_final_g"], dtype=np.float32))
    consts = [make_consts(0, NT), make_consts(1, NT)]
    in_maps = []
    for core in range(8):
        b, c = core // 2, core % 2
        if c == 1:
            xin = np.ascontiguousarray(x[b])
        else:
            xin = np.concatenate([np.zeros((SO, D), np.float32), x[b, :SO]], axis=0)
        m = dict(W)
        m["xin"] = xin
        m["pin"] = np.ascontiguousarray(p[0, b, c * SO:(c + 1) * SO])
        m.update(consts[c])
        in_maps.append(m)
    res = run_bass_kernel_spmd(nc, in_maps, core_ids=list(range(8)))
    out = np.empty((B, S, D), np.float32)
    for core in range(8):
        b, c = core // 2, core % 2
        out[b, c * SO:(c + 1) * SO] = np.asarray(res.results[core]["out"], dtype=np.float32)
    return out
```

```python
import numpy as np
import concourse.bass as bass
import concourse.mybir as mybir
from concourse.bass_utils import run_bass_kernel_spmd

F32 = mybir.dt.float32
BF16 = mybir.dt.bfloat16
AF = mybir.ActivationFunctionType
ALU = mybir.AluOpType
AX = mybir.AxisListType


class Buf:
    __slots__ = ("w", "r", "name")

    def __init__(self, name=""):
        self.w = {}
        self.r = {}
        self.name = name


class TT:
    def __init__(self, t, name=""):
        self.t = t
        self.buf = Buf(name)

    def __getitem__(self, k):
        return self.t[k]


class FW:
    def __init__(self, nc, ndma_sems=6):
        self.nc = nc
        self.eng = {"pe": nc.tensor, "act": nc.scalar, "dve": nc.vector, "pool": nc.gpsimd, "sp": nc.sync}
        self.sem = {k: nc.alloc_semaphore("s_" + k) for k in self.eng}
        self.cnt = {k: 0 for k in self.eng}
        self.waited = {k: {} for k in self.eng}
        self.semobj = {}
        for k, s in self.sem.items():
            self.semobj[id(s)] = s
        self.dq = {}
        for q in ("sp", "pool", "act"):
            sems = [nc.alloc_semaphore("d_%s%d" % (q, i)) for i in range(ndma_sems)]
            for s in sems:
                self.semobj[id(s)] = s
            self.dq[q] = {"sems": sems, "n": 0, "tgt": [0] * ndma_sems}
        self.ninst = 0

    def _wait(self, e, semid, val):
        if val <= 0:
            return
        w = self.waited[e]
        if w.get(semid, 0) >= val:
            return
        self.eng[e].wait_ge(self.semobj[semid], val)
        w[semid] = val

    def _deps(self, e, reads, writes, skip_self=False):
        me = id(self.sem[e])
        for b in reads:
            b = b.buf if isinstance(b, TT) else b
            for s, v in b.w.items():
                if skip_self and s == me:
                    continue
                self._wait(e, s, v)
        for b in writes:
            b = b.buf if isinstance(b, TT) else b
            for s, v in b.w.items():
                if skip_self and s == me:
                    continue
                self._wait(e, s, v)
            for s, v in b.r.items():
                if skip_self and s == me:
                    continue
                self._wait(e, s, v)

    def _mark(self, semid, val, reads, writes):
        for b in reads:
            b = b.buf if isinstance(b, TT) else b
            if b.r.get(semid, 0) < val:
                b.r[semid] = val
        for b in writes:
            b = b.buf if isinstance(b, TT) else b
            if b.w.get(semid, 0) < val:
                b.w[semid] = val

    def op(self, e, fn, reads=(), writes=(), skip_self=None):
        if skip_self is None:
            skip_self = (e == "pe")
        need = self._collect(e, reads, writes, skip_self)
        for semid, val in need[:-1]:
            self.eng[e].wait_ge(self.semobj[semid], val)
        ins = fn(self.eng[e])
        if need:
            semid, val = need[-1]
            ins._wait_ge(self.semobj[semid], val)
        self.cnt[e] += 1
        ins.then_inc(self.sem[e], 1)
        self._mark(id(self.sem[e]), self.cnt[e], reads, writes)
        self.ninst += 1
        return ins

    def _collect(self, e, reads, writes, skip_self):
        me = id(self.sem[e])
        w = self.waited[e]
        need = {}

        def add(s_, v):
            if v <= 0 or (skip_self and s_ == me):
                return
            if w.get(s_, 0) >= v:
                return
            if need.get(s_, 0) < v:
                need[s_] = v
        for b in reads:
            b = b.buf if isinstance(b, TT) else b
            for s_, v in b.w.items():
                add(s_, v)
        for b in writes:
            b = b.buf if isinstance(b, TT) else b
            for s_, v in b.w.items():
                add(s_, v)
            for s_, v in b.r.items():
                add(s_, v)
        for s_, v in need.items():
            w[s_] = v
        return list(need.items())

    def dma(self, q, out, in_, reads=(), writes=(), **kw):
        d = self.dq[q]
        K = len(d["sems"])
        i = d["n"] % K
        s = d["sems"][i]
        self._wait(q, id(s), d["tgt"][i])
        self._deps(q, reads, writes)
        ins = self.eng[q].dma_start(out=out, in_=in_, **kw)
        d["tgt"][i] += 16
        d["n"] += 1
        ins.then_inc(s, 16)
        self._mark(id(s), d["tgt"][i], reads, writes)
        self.ninst += 1
        return ins

    def barrier(self):
        toks = []
        for k in self.eng:
            toks.append((id(self.sem[k]), self.cnt[k]))
        for q, d in self.dq.items():
            for s, t in zip(d["sems"], d["tgt"]):
                toks.append((id(s), t))
        for e in self.eng:
            for s, v in toks:
                self._wait(e, s, v)

    def finish(self, e="sp"):
        for k in self.eng:
            self._wait(e, id(self.sem[k]), self.cnt[k])
        for q, d in self.dq.items():
            for s, t in zip(d["sems"], d["tgt"]):
                self._wait(e, id(s), t)


import numpy as np
from contextlib import ExitStack

D = 2048
KC = 16
C_LX, C_LY, C_Q, C_KV, C_G, C_MA, C_MB = 0, 2048, 4096, 6144, 9216, 9264, 11312
DIN = 13360
EPS = 1e-6
NEGBIG = -30000.0


def build(NT, dbg=False, phases=(1, 2, 3, 4)):
    nc = bass.Bass("TRN2", target_bir_lowering=False)
    fw = FW(nc)
    S = NT * 512
    NOWN = NT // 2
    SO = NOWN * 512
    SUP = min(2, NOWN)
    NSUP = NT // SUP
    L = SUP * 512
    skind = "ExternalOutput" if dbg else "Internal"

    def din(name, shape, dt=F32):
        return nc.dram_tensor(name, list(shape), dt, kind="ExternalInput").ap()

    def dscr(name, shape, dt=BF16):
        return TT(nc.dram_tensor(name, list(shape), dt, kind=skind).ap(), name)

    xin = din("xin", [S, D]); pin = din("pin", [SO, 256])
    w_in = din("w_in", [D, DIN]); b_in = din("b_in", [DIN])
    norm_mix_g = din("norm_mix_g", [D])
    conv_w = din("conv_w", [4, D]); conv_b = din("conv_b", [D])
    w_gate_a = din("w_gate_a", [16, 128, 128]); b_gate_a = din("b_gate_a", [16, 128])
    w_gate_x = din("w_gate_x", [16, 128, 128]); b_gate_x = din("b_gate_x", [16, 128])
    lru_lambda = din("lru_lambda", [D])
    w_lru_out = din("w_lru_out", [D, D])
    cmp_pos = [din("cmp_pos_k", [32, 128]), din("cmp_pos_v", [32, 128])]
    cmp_w1 = [din("cmp_w1_k", [4096, 256]), din("cmp_w1_v", [4096, 256])]
    cmp_w2 = [din("cmp_w2_k", [256, 128]), din("cmp_w2_v", [256, 128])]
    w_attn_out = din("w_attn_out", [D, D]); w_o = din("w_o", [D, D])
    norm_mlp_g = din("norm_mlp_g", [D]); w_up = din("w_up", [D, 4 * D]); w_down = din("w_down", [4 * D, D])
    norm_ple_g = din("norm_ple_g", [D]); w_ple_gate = din("w_ple_gate", [D, D]); w_ple_proj = din("w_ple_proj", [256, D])
    norm_final_g = din("norm_final_g", [D])
    c_ident = din("c_ident", [128, 128]); c_flag = din("c_flag", [128, 1]); c_padbias = din("c_padbias", [128, 1])
    c_fc = din("c_fc", [128, 128]); c_gext = din("c_gext", [128, 256]); c_ov = din("c_ov", [512, 129])
    c_expand = din("c_expand", [128, 64 * 128]); c_gsel = din("c_gsel", [48, 48 * 128]); c_cmpbias = din("c_cmpbias", [128, 4])
    out = nc.dram_tensor("out", [SO, D], F32, kind="ExternalOutput").ap()
    outbuf = Buf("out")

    QT = dscr("QT", [16, 128, SO])
    KcmpT = dscr("KcmpT", [4, 128, S]); VcmpT = dscr("VcmpT", [4, 128, S])
    KslcT = dscr("KslcT", [4, 128, S]); KwinT = dscr("KwinT", [4, 128, S])
    Vslc = dscr("Vslc", [S, 512]); Vwin = dscr("Vwin", [S, 512])
    GT = dscr("GT", [48, SO], F32)
    MAT = dscr("MAT", [16, 128, SO]); MBT = dscr("MBT", [16, 128, SO])
    ZaT = dscr("ZaT", [16, 128, SO]); OT = dscr("OT", [16, 128, SO])


    Wb = {"lru": dscr("Wb_lru", [D, D]), "att": dscr("Wb_att", [D, D]), "o": dscr("Wb_o", [D, D]), "up": dscr("Wb_up", [D, 4 * D]),
          "dn": dscr("Wb_dn", [4 * D, D]), "pg": dscr("Wb_pg", [D, D]), "pp": dscr("Wb_pp", [256, D])}
    conv_list = []
    for nm, src in (("lru", w_lru_out), ("att", w_attn_out), ("o", w_o), ("up", w_up), ("dn", w_down), ("pg", w_ple_gate), ("pp", w_ple_proj)):
        R, Cc = src.shape
        rb = max(1, (1 << 20) // Cc)
        for r0 in range(0, R, rb):
            conv_list.append((nm, src, r0, min(rb, R - r0)))
    conv_pos = [0]

    def conv_step(n):
        for _ in range(n):
            if conv_pos[0] >= len(conv_list):
                return
            nm, src, r0, nr = conv_list[conv_pos[0]]
            conv_pos[0] += 1
            fw.dma("pool", Wb[nm][r0:r0 + nr, :], src[r0:r0 + nr, :], writes=[Wb[nm]])

    def sb(name, shape, dt=F32):
        return TT(nc.alloc_sbuf_tensor(name, list(shape), dt), name)

    PS = [TT(nc.alloc_psum_tensor("ps%d" % i, [128, 512], F32), "ps%d" % i) for i in range(8)]
    psi = [0]

    def ps_next():
        p = PS[psi[0] % 8]
        psi[0] += 1
        return p

    identf = sb("identf", [128, 128]); identb = sb("identb", [128, 128], BF16)
    onesb = sb("onesb", [128, 128], BF16)
    flag = sb("flag", [128, 1]); padbias = sb("padbias", [128, 1])
    fw.dma("sp", identf[:, :], c_ident[:, :], writes=[identf])
    fw.dma("pool", identb[:, :], c_ident[:, :], writes=[identb])
    fw.dma("sp", flag[:, :], c_flag[:, :], writes=[flag])
    fw.dma("sp", padbias[:, :], c_padbias[:, :], writes=[padbias])
    fw.op("dve", lambda e: e.memset(onesb[:, :], 1.0), writes=[onesb])
    epsc = sb("epsc", [128, 1]); onec = sb("onec", [128, 1])
    fw.op("dve", lambda e: e.memset(epsc[:, :], EPS), writes=[epsc])
    fw.op("dve", lambda e: e.memset(onec[:, :], 1.0), writes=[onec])

    rr = {"act_dve": 0}

    def evac_engine():
        rr["act_dve"] += 1
        return "act" if rr["act_dve"] % 2 else "dve"

    def copy_op(e, out_ap, in_ap, reads, writes):
        if e == "act":
            fw.op("act", lambda g: g.copy(out=out_ap, in_=in_ap), reads=reads, writes=writes)
        else:
            fw.op(e, lambda g: g.tensor_copy(out=out_ap, in_=in_ap), reads=reads, writes=writes)

    def norm_transpose(src_ap, gb, hT, col0, xtok, xs, ssq, rstd, keep_x32T=None):
        for tb in range(4):
            fw.dma("sp", xtok[tb][:, :], src_ap[tb * 128:(tb + 1) * 128, :], writes=[xtok[tb]])
            fw.op("act", lambda e: e.activation(out=xs[tb][:, :], in_=xtok[tb][:, :], func=AF.Square,
                                                accum_out=ssq[tb][:, 0:1]),
                  reads=[xtok[tb]], writes=[xs[tb], ssq[tb]])
            fw.op("act", lambda e: e.activation(out=rstd[tb][:, :], in_=ssq[tb][:, :], func=AF.Sqrt, scale=1.0 / D, bias=epsc[:, 0:1]),
                  reads=[ssq[tb], epsc], writes=[rstd[tb]])
            fw.op("dve", lambda e: e.reciprocal(out=rstd[tb][:, :], in_=rstd[tb][:, :]), reads=[rstd[tb]], writes=[rstd[tb]])
            fw.op("dve", lambda e: e.scalar_tensor_tensor(out=xs[tb][:, :], in0=xtok[tb][:, :], scalar=rstd[tb][:, 0:1],
                                                          in1=gb[:, :], op0=ALU.mult, op1=ALU.mult),
                  reads=[xtok[tb], rstd[tb], gb], writes=[xs[tb]])
        for i in range(8):
            p = ps_next()
            pb = p.t[:, :].bitcast(BF16)
            for cc in range(2):
                c = 2 * i + cc
                for tb in range(4):
                    fw.op("pe", lambda e: e.transpose(pb[:, cc * 512 + tb * 128: cc * 512 + tb * 128 + 128],
                                                      xs[tb][:, c * 128:(c + 1) * 128], identb[:, :]),
                          reads=[xs[tb], identb], writes=[p])
            copy_op(evac_engine(), hT[:, 2 * i:2 * i + 2, col0:col0 + 512],
                    pb.rearrange("p (a b) -> p a b", a=2), [p], [hT])

    if 1 in phases:
        with ExitStack() as es:
            def sbp(name, shape, dt=F32):
                return TT(es.enter_context(nc.sbuf_tensor(name, list(shape), dt)), name)
            gb = sbp("gb", [128, D])
            fw.dma("sp", gb[:, :], norm_mix_g.rearrange("(o d) -> o d", o=1).partition_broadcast(128), writes=[gb])
            xtok = [sbp("xtok%d" % i, [128, D]) for i in range(4)]
            xs = [sbp("xs%d" % i, [128, D], BF16) for i in range(4)]
            ssq = [sbp("ssq%d" % i, [128, 1]) for i in range(4)]
            rstd = [sbp("rstd%d" % i, [128, 1]) for i in range(4)]
            hT = sbp("hT", [128, KC, L], BF16)
            wslab = [sbp("wslab%d" % i, [128, KC, 512], BF16) for i in range(2)]
            wsi = [0]
            bcol = sbp("bcol", [128, 128])
            convw = sbp("convw", [128, 4, 16]); convb = sbp("convb", [128, 16])
            bga = sbp("bga", [128, 16]); bgx = sbp("bgx", [128, 16]); lam = sbp("lam", [128, 16])
            cs1 = sbp("cs1", [128, 16]); cs2 = sbp("cs2", [128, 16])
            wga = sbp("wga", [128, 16, 128], BF16); wgx = sbp("wgx", [128, 16, 128], BF16)
            vbias = [sbp("vbias%d" % i, [128, 512]) for i in range(2)]
            state = sbp("state", [128, 16]); carry = sbp("carry", [128, 16, 3])
            bqs = sbp("bqs", [128, 16])
            chunks = []
            def addch(c0, n):
                for i in range(n):
                    chunks.append(c0 + 128 * i)
            sec_starts = [(C_LX, 16), (C_LY, 16), (C_Q, 16), (C_KV, 24), (C_MA, 16), (C_MB, 16)]
            bidx = {}
            k = 0
            for c0, n in sec_starts:
                src = b_in[c0:c0 + 128 * n].rearrange("(c p) -> p c", p=128)
                fw.dma("sp", bcol[:, k:k + n], src, writes=[bcol], allow_slow_non_contiguous=True)
                for i in range(n):
                    bidx[c0 + 128 * i] = k + i
                k += n
            fw.dma("sp", bcol[0:48, k:k + 1], b_in[C_G:C_G + 48].rearrange("(p o) -> p o", o=1), writes=[bcol])
            bidx[C_G] = k
            fw.dma("sp", convw[:, :, :], conv_w.rearrange("k (c p) -> p k c", p=128), writes=[convw], allow_slow_non_contiguous=True)
            fw.dma("sp", convb[:, :], conv_b.rearrange("(c p) -> p c", p=128), writes=[convb], allow_slow_non_contiguous=True)
            fw.dma("sp", bga[:, :], b_gate_a.rearrange("c p -> p c"), writes=[bga], allow_slow_non_contiguous=True)
            fw.dma("sp", bgx[:, :], b_gate_x.rearrange("c p -> p c"), writes=[bgx], allow_slow_non_contiguous=True)
            fw.dma("sp", lam[:, :], lru_lambda.rearrange("(c p) -> p c", p=128), writes=[lam], allow_slow_non_contiguous=True)
            fw.dma("pool", wga[:, :, :], w_gate_a.rearrange("n c d -> c n d"), writes=[wga])
            fw.dma("pool", wgx[:, :, :], w_gate_x.rearrange("n c d -> c n d"), writes=[wgx])
            fw.dma("sp", vbias[0][:, :], b_in[C_KV + 3 * 512:C_KV + 4 * 512].rearrange("(o d) -> o d", o=1).partition_broadcast(128), writes=[vbias[0]])
            fw.dma("sp", vbias[1][:, :], b_in[C_KV + 5 * 512:C_KV + 6 * 512].rearrange("(o d) -> o d", o=1).partition_broadcast(128), writes=[vbias[1]])
            fw.op("act", lambda e: e.activation(out=cs1[:, :], in_=lam[:, :], func=AF.Exp, scale=-1.0), reads=[lam], writes=[cs1])
            fw.op("act", lambda e: e.activation(out=cs1[:, :], in_=cs1[:, :], func=AF.Ln, bias=onec[:, 0:1]), reads=[cs1, onec], writes=[cs1])
            fw.op("dve", lambda e: e.tensor_scalar(out=cs2[:, :], in0=cs1[:, :], scalar1=-16.0, scalar2=None, op0=ALU.mult), reads=[cs1], writes=[cs2])
            fw.op("dve", lambda e: e.tensor_scalar(out=cs1[:, :], in0=cs1[:, :], scalar1=-8.0, scalar2=None, op0=ALU.mult), reads=[cs1], writes=[cs1])
            fw.op("dve", lambda e: e.tensor_scalar(out=bqs[:, :], in0=bcol[:, bidx[C_Q]:bidx[C_Q] + 16], scalar1=128.0 ** -0.5, scalar2=None, op0=ALU.mult), reads=[bcol], writes=[bqs])
            fw.op("dve", lambda e: e.memset(state[:, :], 0.0), writes=[state])
            fw.op("dve", lambda e: e.memset(carry[:, :, :], 0.0), writes=[carry])
            ulx = sbp("ulx", [128, 3 + L]); xc = sbp("xc", [128, L]); xcb = sbp("xcb", [128, L], BF16)
            ra = sbp("ra", [128, L]); aa = sbp("aa", [128, L]); ri = sbp("ri", [128, L]); gly = sbp("gly", [128, L])
            zb = sbp("zb", [128, L], BF16)
            otile = [sbp("otile%d" % i, [128, 512], BF16) for i in range(4)]
            oti = [0]
            gtile = [sbp("gtile%d" % i, [48, 512]) for i in range(2)]
            vtile = [sbp("vtile%d" % i, [128, 512], BF16) for i in range(2)]

            def load_slab(c0, ncols):
                w = wslab[wsi[0] % 2]
                wsi[0] += 1
                conv_step(2)
                fw.dma("pool", w[:, :, 0:ncols], w_in[:, c0:c0 + ncols].rearrange("(kc p) n -> p kc n", p=128), writes=[w])
                return w

            def proj_fm(w, cc, ncol, j):
                p = ps_next()
                for kc in range(KC):
                    fw.op("pe", lambda e: e.matmul(p[0:ncol, :], w[:, kc, cc * 128:cc * 128 + ncol], hT[:, kc, j * 512:(j + 1) * 512],
                                                   start=(kc == 0), stop=(kc == KC - 1)), reads=[w, hT], writes=[p])
                return p

            def next_otile():
                t = otile[oti[0] % 4]
                oti[0] += 1
                return t

            for s in range(NSUP):
                own = s >= NSUP // 2
                tok0 = s * L
                own0 = tok0 - SO
                for j in range(SUP):
                    norm_transpose(xin[tok0 + j * 512: tok0 + (j + 1) * 512, :], gb, hT, j * 512, xtok, xs, ssq, rstd)
                for sl in range(4):
                    wlx = load_slab(C_LX + 512 * sl, 512)
                    wly = load_slab(C_LY + 512 * sl, 512) if own else None
                    for cc in range(4):
                        n = 4 * sl + cc
                        fw.op("dve", lambda e: e.tensor_copy(out=ulx[:, 0:3], in_=carry[:, n, :]), reads=[carry], writes=[ulx])
                        for j in range(SUP):
                            p = proj_fm(wlx, cc, 128, j)
                            fw.op("act", lambda e: e.activation(out=ulx[:, 3 + j * 512:3 + (j + 1) * 512], in_=p[:, :], func=AF.Identity,
                                                                bias=bcol[:, bidx[C_LX] + n:bidx[C_LX] + n + 1]), reads=[p, bcol], writes=[ulx])
                            if own:
                                p2 = proj_fm(wly, cc, 128, j)
                                fw.op("act", lambda e: e.activation(out=gly[:, j * 512:(j + 1) * 512], in_=p2[:, :], func=AF.Gelu,
                                                                    bias=bcol[:, bidx[C_LY] + n:bidx[C_LY] + n + 1]), reads=[p2, bcol], writes=[gly])
                        if s == NSUP // 2 - 1:
                            fw.op("dve", lambda e: e.tensor_scalar(out=carry[:, n, :], in0=ulx[:, L:L + 3], scalar1=flag[:, 0:1], scalar2=None, op0=ALU.mult),
                                  reads=[ulx, flag], writes=[carry])
                        else:
                            fw.op("dve", lambda e: e.tensor_copy(out=carry[:, n, :], in_=ulx[:, L:L + 3]), reads=[ulx], writes=[carry])
                        fw.op("dve", lambda e: e.tensor_scalar(out=xc[:, :], in0=ulx[:, 3:3 + L], scalar1=convw[:, 3, n:n + 1], scalar2=convb[:, n:n + 1],
                                                               op0=ALU.mult, op1=ALU.add), reads=[ulx, convw, convb], writes=[xc])
                        for k in range(3):
                            fw.op("dve", lambda e: e.scalar_tensor_tensor(out=xc[:, :], in0=ulx[:, k:k + L], scalar=convw[:, k, n:n + 1], in1=xc[:, :],
                                                                          op0=ALU.mult, op1=ALU.add), reads=[ulx, convw, xc], writes=[xc])
                        fw.op("act", lambda e: e.copy(out=xcb[:, :], in_=xc[:, :]), reads=[xc], writes=[xcb])
                        for j in range(SUP):
                            p = ps_next()
                            fw.op("pe", lambda e: e.matmul(p[:, :], wga[:, n, :], xcb[:, j * 512:(j + 1) * 512], start=True, stop=True), reads=[wga, xcb], writes=[p])
                            fw.op("act", lambda e: e.activation(out=ra[:, j * 512:(j + 1) * 512], in_=p[:, :], func=AF.Sigmoid, bias=bga[:, n:n + 1]),
                                  reads=[p, bga], writes=[ra])
                            p = ps_next()
                            fw.op("pe", lambda e: e.matmul(p[:, :], wgx[:, n, :], xcb[:, j * 512:(j + 1) * 512], start=True, stop=True), reads=[wgx, xcb], writes=[p])
                            fw.op("act", lambda e: e.activation(out=ri[:, j * 512:(j + 1) * 512], in_=p[:, :], func=AF.Sigmoid, bias=bgx[:, n:n + 1]),
                                  reads=[p, bgx], writes=[ri])
                        fw.op("act", lambda e: e.activation(out=aa[:, :], in_=ra[:, :], func=AF.Exp, scale=cs1[:, n:n + 1]), reads=[ra, cs1], writes=[aa])
                        fw.op("act", lambda e: e.activation(out=ra[:, :], in_=ra[:, :], func=AF.Exp, scale=cs2[:, n:n + 1]), reads=[ra, cs2], writes=[ra])
                        fw.op("act", lambda e: e.activation(out=ra[:, :], in_=ra[:, :], func=AF.Sqrt, scale=-1.0, bias=onec[:, 0:1]), reads=[ra, onec], writes=[ra])
                        fw.op("dve", lambda e: e.tensor_tensor(out=ri[:, :], in0=ri[:, :], in1=xc[:, :], op=ALU.mult), reads=[ri, xc], writes=[ri])
                        fw.op("dve", lambda e: e.tensor_tensor(out=ri[:, :], in0=ri[:, :], in1=ra[:, :], op=ALU.mult), reads=[ri, ra], writes=[ri])
                        fw.op("dve", lambda e: e.tensor_tensor_scan(out=xc[:, :], data0=aa[:, :], data1=ri[:, :], initial=state[:, n:n + 1],
                                                                    op0=ALU.mult, op1=ALU.add), reads=[aa, ri, state], writes=[xc])
                        if s == NSUP // 2 - 1:
                            fw.op("dve", lambda e: e.tensor_scalar(out=state[:, n:n + 1], in0=xc[:, L - 1:L], scalar1=flag[:, 0:1], scalar2=None, op0=ALU.mult),
                                  reads=[xc, flag], writes=[state])
                        else:
                            fw.op("dve", lambda e: e.tensor_copy(out=state[:, n:n + 1], in_=xc[:, L - 1:L]), reads=[xc], writes=[state])
                        if own:
                            fw.op("dve", lambda e: e.tensor_tensor(out=zb[:, :], in0=gly[:, :], in1=xc[:, :], op=ALU.mult), reads=[gly, xc], writes=[zb])
                            fw.dma("sp", ZaT[n, :, own0:own0 + L], zb[:, :], reads=[zb], writes=[ZaT])
                for (sec, dst) in ((0, KcmpT), (1, VcmpT), (2, KslcT), (4, KwinT)):
                    w = load_slab(C_KV + 512 * sec, 512)
                    for cc in range(4):
                        for j in range(SUP):
                            p = proj_fm(w, cc, 128, j)
                            t = next_otile()
                            bi = bidx[C_KV] + 4 * sec + cc
                            fw.op("act", lambda e: e.activation(out=t[:, :], in_=p[:, :], func=AF.Identity, bias=bcol[:, bi:bi + 1]), reads=[p, bcol], writes=[t])
                            fw.dma("sp", dst[cc, :, tok0 + j * 512: tok0 + (j + 1) * 512], t[:, :], reads=[t], writes=[dst])
                for (vi, sec, dst) in ((0, 3, Vslc), (1, 5, Vwin)):
                    w = load_slab(C_KV + 512 * sec, 512)
                    for tb in range(SUP * 4):
                        p = ps_next()
                        for kc in range(KC):
                            fw.op("pe", lambda e: e.matmul(p[:, :], hT[:, kc, tb * 128:(tb + 1) * 128], w[:, kc, :], start=(kc == 0), stop=(kc == KC - 1)),
                                  reads=[w, hT], writes=[p])
                        t = vtile[tb % 2]
                        fw.op("dve", lambda e: e.tensor_tensor(out=t[:, :], in0=p[:, :], in1=vbias[vi][:, :], op=ALU.add), reads=[p, vbias[vi]], writes=[t])
                        fw.dma("sp", dst[tok0 + tb * 128: tok0 + (tb + 1) * 128, :], t[:, :], reads=[t], writes=[dst])
                if not own:
                    continue
                for sl in range(4):
                    w = load_slab(C_Q + 512 * sl, 512)
                    for cc in range(4):
                        hd = 4 * sl + cc
                        for j in range(SUP):
                            p = proj_fm(w, cc, 128, j)
                            t = next_otile()
                            fw.op("act", lambda e: e.activation(out=t[:, :], in_=p[:, :], func=AF.Identity, bias=bqs[:, hd:hd + 1], scale=128.0 ** -0.5),
                                  reads=[p, bqs], writes=[t])
                            fw.dma("sp", QT[hd, :, own0 + j * 512: own0 + (j + 1) * 512], t[:, :], reads=[t], writes=[QT])
                w = load_slab(C_G, 48)
                for j in range(SUP):
                    p = proj_fm(w, 0, 48, j)
                    t = gtile[j % 2]
                    fw.op("act", lambda e: e.activation(out=t[:, :], in_=p[0:48, :], func=AF.Sigmoid, bias=bcol[0:48, bidx[C_G]:bidx[C_G] + 1]),
                          reads=[p, bcol], writes=[t])
                    fw.dma("sp", GT[:, own0 + j * 512: own0 + (j + 1) * 512], t[:, :], reads=[t], writes=[GT])
                for (c0, dst) in ((C_MA, MAT), (C_MB, MBT)):
                    for sl in range(4):
                        w = load_slab(c0 + 512 * sl, 512)
                        for cc in range(4):
                            n = 4 * sl + cc
                            for j in range(SUP):
                                p = proj_fm(w, cc, 128, j)
                                t = next_otile()
                                fw.op("act", lambda e: e.activation(out=t[:, :], in_=p[:, :], func=AF.Sigmoid, bias=bcol[:, bidx[c0] + n:bidx[c0] + n + 1]),
                                      reads=[p, bcol], writes=[t])
                                fw.dma("sp", dst[n, :, own0 + j * 512: own0 + (j + 1) * 512], t[:, :], reads=[t], writes=[dst])
            fw.barrier()


    NCB = S // 16 - 1
    NBC = (NCB + 127) // 128
    kcT = [sb("kcT%d" % g, [128, 512], BF16) for g in range(4)]
    vc = [sb("vc%d" % g, [128, 4, 128], BF16) for g in range(4)]
    ACC = PS[4:8]
    wk = [0]

    def ps_work():
        p = PS[wk[0] % 4]
        wk[0] += 1
        return p

    if 2 in phases:
        with ExitStack() as es:
            def sbp(name, shape, dt=F32):
                return TT(es.enter_context(nc.sbuf_tensor(name, list(shape), dt)), name)
            w1s = sbp("w1s", [128, 32, 256], BF16); w2s = sbp("w2s", [128, 2, 128], BF16)
            pos32 = sbp("pos32", [32, 128]); posT = sbp("posT", [128, 32], BF16); cb = sbp("cb", [128, 2])
            KT = [sbp("KTc%d" % i, [128, S], BF16) for i in range(2)]
            hid = [sbp("hid%d" % i, [128, 512], BF16) for i in range(2)]
            for g in range(4):
                fw.op("dve", lambda e: e.memset(kcT[g][:, :], 0.0), writes=[kcT[g]])
                fw.op("dve", lambda e: e.memset(vc[g][:, :, :], 0.0), writes=[vc[g]])
            fw.op("dve", lambda e: e.memset(hid[0][:, :], 0.0), writes=[hid[0]])
            fw.op("dve", lambda e: e.memset(hid[1][:, :], 0.0), writes=[hid[1]])
            for kv in range(2):
                for i in range(4):
                    fw.dma("pool", w1s[:, 8 * i:8 * i + 8, :], cmp_w1[kv][1024 * i:1024 * (i + 1), :].rearrange("(pos d) h -> d pos h", d=128), writes=[w1s])
                fw.dma("pool", w2s[:, :, :], cmp_w2[kv].rearrange("(hc p) d -> p hc d", p=128), writes=[w2s])
                fw.dma("sp", pos32[:, :], cmp_pos[kv][:, :], writes=[pos32])
                p = ps_work()
                fw.op("pe", lambda e: e.transpose(p[:, 0:32], pos32[:, :], identf[0:32, 0:32]), reads=[pos32, identf], writes=[p])
                fw.op("dve", lambda e: e.tensor_copy(out=posT[:, :], in_=p[:, 0:32]), reads=[p], writes=[posT])
                for hc in range(2):
                    p = ps_work()
                    for pos in range(32):
                        fw.op("pe", lambda e: e.matmul(p[:, 0:1], w1s[:, pos, hc * 128:(hc + 1) * 128], posT[:, pos:pos + 1], start=(pos == 0), stop=(pos == 31)),
                              reads=[w1s, posT], writes=[p])
                    fw.op("dve", lambda e: e.tensor_copy(out=cb[:, hc:hc + 1], in_=p[:, 0:1]), reads=[p], writes=[cb])
                src = KcmpT if kv == 0 else VcmpT
                for g in range(4):
                    kt = KT[g % 2]
                    fw.dma("sp", kt[:, :], src[g, :, :], reads=[src], writes=[kt])
                    for hc in range(2):
                        p = ps_work()
                        for pos in range(32):
                            fw.op("pe", lambda e: e.matmul(p[:, 0:NCB], w1s[:, pos, hc * 128:(hc + 1) * 128], kt[:, pos:pos + 16 * (NCB - 1) + 1:16],
                                                           start=(pos == 0), stop=(pos == 31)), reads=[w1s, kt], writes=[p])
                        fw.op("act", lambda e: e.activation(out=hid[hc][:, 0:NCB], in_=p[:, 0:NCB], func=AF.Gelu, bias=cb[:, hc:hc + 1]), reads=[p, cb], writes=[hid[hc]])
                    if kv == 0:
                        p = ps_work()
                        for hc in range(2):
                            fw.op("pe", lambda e: e.matmul(p[:, 0:NCB], w2s[:, hc, :], hid[hc][:, 0:NCB], start=(hc == 0), stop=(hc == 1)), reads=[w2s, hid[hc]], writes=[p])
                        fw.op("dve", lambda e: e.tensor_copy(out=kcT[g][:, 0:NCB], in_=p[:, 0:NCB]), reads=[p], writes=[kcT[g]])
                    else:
                        for bc in range(NBC):
                            nb = min(128, NCB - bc * 128)
                            p = ps_work()
                            for hc in range(2):
                                fw.op("pe", lambda e: e.matmul(p[0:nb, 0:128], hid[hc][:, bc * 128:bc * 128 + nb], w2s[:, hc, :], start=(hc == 0), stop=(hc == 1)),
                                      reads=[w2s, hid[hc]], writes=[p])
                            fw.op("dve", lambda e: e.tensor_copy(out=vc[g][0:nb, bc, :], in_=p[0:nb, 0:128]), reads=[p], writes=[vc[g]])
            fw.barrier()

    if 3 in phases:
        with ExitStack() as es:
            def sbp(name, shape, dt=F32):
                return TT(es.enter_context(nc.sbuf_tensor(name, list(shape), dt)), name)
            NKC = S // 128
            ovx = sbp("ovx", [128, 4, 129], BF16); fc = sbp("fc", [128, 128]); gext = sbp("gext", [128, 256])
            expand = sbp("expand", [128, 64, 128], BF16); gsel = sbp("gsel", [48, 48, 128], BF16)
            cmpbias = sbp("cmpbias", [128, 4])
            fw.dma("pool", ovx[:, :, :], c_ov.rearrange("(c p) n -> p c n", p=128), writes=[ovx])
            fw.dma("sp", fc[:, :], c_fc[:, :], writes=[fc]); fw.dma("sp", gext[:, :], c_gext[:, :], writes=[gext])
            for i in range(4):
                fw.dma("pool", expand[:, 16 * i:16 * i + 16, :], c_expand[:, 2048 * i:2048 * (i + 1)].rearrange("p (a b) -> p a b", b=128), writes=[expand])
            fw.dma("pool", gsel[:, :, :], c_gsel.rearrange("p (a b) -> p a b", b=128), writes=[gsel])
            fw.dma("sp", cmpbias[:, :], c_cmpbias[:, :], writes=[cmpbias])
            KsT = sbp("KsT", [128, S], BF16); KwT = sbp("KwT", [128, S], BF16)
            Vs = sbp("Vs", [128, NKC, 128], BF16); Vw = sbp("Vw", [128, NKC, 128], BF16)
            qT = [sbp("qT%d" % i, [128, 4, 512], BF16) for i in range(2)]
            g32 = [sbp("g32_%d" % i, [48, 512]) for i in range(2)]; gTb = [sbp("gTb%d" % i, [48, 512], BF16) for i in range(2)]
            pT = [sbp("pT%d" % i, [128, 512], BF16) for i in range(8)]
            sacc = [sbp("sacc%d" % i, [128, 512]) for i in range(4)]
            onesf = sbp("onesf", [128, 128])
            fw.op("dve", lambda e: e.memset(onesf[:, :], 1.0), writes=[onesf])
            pti = [0]
            oacc = [sbp("oacc%d" % i, [128, 512]) for i in range(4)]
            otmp = [sbp("otmp%d" % i, [128, 512]) for i in range(2)]
            rs = [sbp("rs%d" % i, [128, 512]) for i in range(2)]
            wgt = [sbp("wgt%d" % i, [128, 512]) for i in range(2)]
            impacc = [sbp("impacc%d" % i, [128, 128]) for i in range(4)]
            rec = [sbp("rec%d" % i, [128, 1]) for i in range(4)]
            score = [sbp("score%d" % i, [128, 128]) for i in range(2)]
            scw = [sbp("scw%d" % i, [128, 128]) for i in range(2)]
            m8 = [sbp("m8_%d" % i, [128, 16]) for i in range(2)]
            maskq = [sbp("maskq%d" % i, [128, 128], BF16) for i in range(2)]
            maskT = sbp("maskT", [128, 512], BF16)
            obf = [sbp("obf%d" % i, [128, 512], BF16) for i in range(2)]
            cnt = {"fin": 0, "ob": 0, "sum": 0, "set": 0}

            def next_pT():
                t = pT[pti[0] % 8]
                pti[0] += 1
                return t

            SUM_ON_PE = True

            def finalize(h, br, o_ps, sa, gt, first):
                i = cnt["fin"] % 2
                cnt["fin"] += 1
                if SUM_ON_PE:
                    s_ps = sa
                else:
                    s_ps = ps_work()
                    fw.op("pe", lambda e: e.matmul(s_ps[:, :], onesf[:, :], sa[:, :], start=True, stop=True), reads=[onesf, sa], writes=[s_ps])
                fw.op("dve", lambda e: e.tensor_scalar(out=rs[i][:, :], in0=s_ps[:, :], scalar1=1e-30, scalar2=None, op0=ALU.max), reads=[s_ps], writes=[rs[i]])
                fw.op("dve", lambda e: e.reciprocal(out=rs[i][:, :], in_=rs[i][:, :]), reads=[rs[i]], writes=[rs[i]])
                pg = ps_work()
                fw.op("pe", lambda e: e.matmul(pg[:, :], gsel[:, br * 16 + h, :], gt[:, :], start=True, stop=True), reads=[gsel, gt], writes=[pg])
                fw.op("dve", lambda e: e.tensor_tensor(out=wgt[i][:, :], in0=rs[i][:, :], in1=pg[:, :], op=ALU.mult), reads=[rs[i], pg], writes=[wgt[i]])
                hl_ = h % 4
                if first:
                    fw.op("dve", lambda e: e.tensor_tensor(out=oacc[hl_][:, :], in0=o_ps[:, :], in1=wgt[i][:, :], op=ALU.mult), reads=[o_ps, wgt[i]], writes=[oacc[hl_]])
                else:
                    fw.op("dve", lambda e: e.tensor_tensor(out=otmp[i][:, :], in0=o_ps[:, :], in1=wgt[i][:, :], op=ALU.mult), reads=[o_ps, wgt[i]], writes=[otmp[i]])
                    fw.op("pool", lambda e: e.tensor_tensor(out=oacc[hl_][:, :], in0=oacc[hl_][:, :], in1=otmp[i][:, :], op=ALU.add), reads=[oacc[hl_], otmp[i]], writes=[oacc[hl_]])

            LA = 4

            def pipeline(items):
                n = len(items)
                for i in range(min(LA, n)):
                    items[i][0]()
                for i in range(n):
                    items[i][1]()
                    if i + LA < n:
                        items[i + LA][0]()

            def make_item(score_mms, exp_bias, selects, acc_mms, post=None, sum_acc=None):
                st = {"last": post is not None}

                def A():
                    p = ps_work()
                    t = next_pT()
                    st["t"] = t
                    nmm = len(score_mms)
                    for i, (l, r, rd) in enumerate(score_mms):
                        fw.op("pe", lambda e: e.matmul(p[:, :], l, r, start=(i == 0), stop=(i == nmm - 1)), reads=rd, writes=[p])
                    if exp_bias is None:
                        fw.op("act", lambda e: e.activation(out=t[:, :], in_=p[:, :], func=AF.Exp), reads=[p], writes=[t])
                    else:
                        fw.op("act", lambda e: e.activation(out=t[:, :], in_=p[:, :], func=AF.Exp, bias=exp_bias[0]), reads=[p, exp_bias[1]], writes=[t])
                    for (pat, base, cm) in selects:
                        fw.op("pool", lambda e: e.affine_select(out=t[:, :], in_=t[:, :], pattern=pat, compare_op=ALU.is_ge, fill=0.0, base=base, channel_multiplier=cm),
                              reads=[t], writes=[t])

                def B():
                    t = st["t"]
                    for (o_tt, fn, rd) in acc_mms:
                        fw.op("pe", lambda e: fn(e, t), reads=rd + [t], writes=[o_tt])
                    if sum_acc is not None and SUM_ON_PE:
                        sa, first_ = sum_acc
                        fw.op("pe", lambda e: e.matmul(sa[:, :], onesb[:, :], t[:, :], start=first_, stop=st.get("last", False)), reads=[onesb, t], writes=[sa])
                    elif sum_acc is not None:
                        sa, first_ = sum_acc
                        eng_ = "dve" if cnt["sum"] % 2 == 0 else "pool"
                        cnt["sum"] += 1
                        if first_:
                            fw.op(eng_, lambda e: e.tensor_copy(out=sa[:, :], in_=t[:, :]), reads=[t], writes=[sa])
                        else:
                            fw.op(eng_, lambda e: e.tensor_tensor(out=sa[:, :], in0=sa[:, :], in1=t[:, :], op=ALU.add), reads=[sa, t], writes=[sa])
                    if post is not None:
                        post()
                return (A, B)

            for g in range(4):
                fw.dma("sp", KsT[:, :], KslcT[g, :, :], reads=[KslcT], writes=[KsT])
                fw.dma("sp", KwT[:, :], KwinT[g, :, :], reads=[KwinT], writes=[KwT])
                for i in range(0, NKC, 16):
                    n = min(16, NKC - i)
                    fw.dma("sp", Vs[:, i:i + n, :], Vslc[i * 128:(i + n) * 128, g * 128:(g + 1) * 128].rearrange("(kc p) d -> p kc d", p=128), reads=[Vslc], writes=[Vs])
                    fw.dma("sp", Vw[:, i:i + n, :], Vwin[i * 128:(i + n) * 128, g * 128:(g + 1) * 128].rearrange("(kc p) d -> p kc d", p=128), reads=[Vwin], writes=[Vw])
                for qt in range(NOWN):
                    q0 = SO + qt * 512
                    qq = qT[qt % 2]; gf = g32[qt % 2]; gt = gTb[qt % 2]
                    fw.dma("sp", qq[:, :, :], QT[4 * g:4 * g + 4, :, qt * 512:(qt + 1) * 512].rearrange("h p t -> p h t"), reads=[QT], writes=[qq])
                    fw.dma("sp", gf[:, :], GT[:, qt * 512:(qt + 1) * 512], reads=[GT], writes=[gf])
                    fw.op("act", lambda e: e.copy(out=gt[:, :], in_=gf[:, :]), reads=[gf], writes=[gt])
                    items = []
                    ncc = min(NBC, ((q0 + 480) // 16) // 128 + 1)
                    o_ps, i_ps0, i_ps1 = ACC[0], ACC[2], ACC[3]

                    def cmp_post(h):
                        def f():
                            finalize(4 * g + h, 0, o_ps, (ACC[1] if SUM_ON_PE else sacc[h % 2]), gt, True)
                            for qs in range(4):
                                ip = i_ps0 if qs < 2 else i_ps1
                                c0 = (qs % 2) * 129
                                fw.op("dve", lambda e: e.tensor_scalar(out=rec[qs][:, :], in0=ip[:, c0 + 128:c0 + 129], scalar1=1e-30, scalar2=None, op0=ALU.max), reads=[ip], writes=[rec[qs]])
                                fw.op("dve", lambda e: e.reciprocal(out=rec[qs][:, :], in_=rec[qs][:, :]), reads=[rec[qs]], writes=[rec[qs]])
                                if h == 0:
                                    fw.op("dve", lambda e: e.tensor_scalar(out=impacc[qs][:, :], in0=ip[:, c0:c0 + 128], scalar1=rec[qs][:, 0:1], scalar2=None, op0=ALU.mult),
                                          reads=[ip, rec[qs]], writes=[impacc[qs]])
                                else:
                                    fw.op("dve", lambda e: e.scalar_tensor_tensor(out=impacc[qs][:, :], in0=ip[:, c0:c0 + 128], scalar=rec[qs][:, 0:1], in1=impacc[qs][:, :],
                                                                                  op0=ALU.mult, op1=ALU.add), reads=[ip, rec[qs], impacc[qs]], writes=[impacc[qs]])
                        return f

                    for h in range(4):
                        for cbk in range(ncc):
                            sel = []
                            if (cbk * 128 + 127) * 16 + 31 > q0:
                                sel.append(([[1, 512]], q0 - cbk * 2048 - 31, -16))
                            st_, sp_ = (cbk == 0), (cbk == ncc - 1)
                            accm = [(o_ps, (lambda e, t, cbk=cbk, st_=st_, sp_=sp_: e.matmul(o_ps[:, :], vc[g][:, cbk, :], t[:, :], start=st_, stop=sp_)), [vc[g]])]
                            for qs in range(4):
                                ip = i_ps0 if qs < 2 else i_ps1
                                accm.append((ip, (lambda e, t, ip=ip, qs=qs, cbk=cbk, st_=st_, sp_=sp_: e.matmul(ip[:, (qs % 2) * 129:(qs % 2) * 129 + 129], t[:, qs * 128:(qs + 1) * 128],
                                                                                                         ovx[:, cbk, :], start=st_, stop=sp_)), [ovx]))
                            items.append(make_item([(kcT[g][:, cbk * 128:(cbk + 1) * 128], qq[:, h, :], [kcT[g], qq])], (cmpbias[:, cbk:cbk + 1], cmpbias), sel, accm,
                                                   cmp_post(h) if cbk == ncc - 1 else None, sum_acc=((ACC[1] if SUM_ON_PE else sacc[h % 2]), st_)))
                    pipeline(items)
                    pm = ps_work()
                    pmb = pm.t[:, :].bitcast(BF16)
                    for qs in range(4):
                        i = qs % 2
                        off = 128 - (q0 + qs * 128) // 64
                        fw.op("dve", lambda e: e.tensor_tensor(out=score[i][:, :], in0=impacc[qs][:, :], in1=fc[:, :], op=ALU.add), reads=[impacc[qs], fc], writes=[score[i]])
                        fw.op("dve", lambda e: e.tensor_tensor(out=score[i][:, :], in0=score[i][:, :], in1=gext[:, off:off + 128], op=ALU.add), reads=[score[i], gext], writes=[score[i]])
                        fw.op("dve", lambda e: e.max(out=m8[i][:, 0:8], in_=score[i][:, :]), reads=[score[i]], writes=[m8[i]])
                        fw.op("dve", lambda e: e.match_replace(out=scw[i][:, :], in_to_replace=m8[i][:, 0:8], in_values=score[i][:, :], imm_value=-1e30),
                              reads=[score[i], m8[i]], writes=[scw[i]])
                        fw.op("dve", lambda e: e.max(out=m8[i][:, 8:16], in_=scw[i][:, :]), reads=[scw[i]], writes=[m8[i]])
                        fw.op("dve", lambda e: e.tensor_scalar(out=maskq[i][:, :], in0=score[i][:, :], scalar1=m8[i][:, 15:16], scalar2=NEGBIG, op0=ALU.is_lt, op1=ALU.mult),
                              reads=[score[i], m8[i]], writes=[maskq[i]])
                        fw.op("pe", lambda e: e.transpose(pmb[:, qs * 128:(qs + 1) * 128], maskq[i][:, :], identb[:, :]), reads=[maskq[i], identb], writes=[pm])
                    fw.op("act", lambda e: e.copy(out=maskT[:, :], in_=pmb[:, 0:512]), reads=[pm], writes=[maskT])
                    nks = (q0 + 512) // 128
                    kc0 = (q0 - 512) // 128
                    for hp in range(2):
                        hs = (2 * hp, 2 * hp + 1)
                        items = []
                        for br in (1, 2):
                            sset = cnt["set"] % 2
                            cnt["set"] += 1
                            accs = ({hs[0]: (ACC[0], ACC[1]), hs[1]: (ACC[2], ACC[3])} if SUM_ON_PE else
                                    {hs[0]: (ACC[2 * sset], sacc[2 * sset]), hs[1]: (ACC[2 * sset + 1], sacc[2 * sset + 1])})
                            kfirst = 0 if br == 1 else kc0
                            for kc in range(kfirst, nks):
                                for h in hs:
                                    o_a, s_a = accs[h]
                                    st_, sp_ = (kc == kfirst), (kc == nks - 1)
                                    if br == 1:
                                        sel = [([[1, 512]], q0 - kc * 128, -1)] if kc * 128 >= q0 else []
                                        vv = Vs
                                        smm = [(expand[:, kc, :], maskT[:, :], [expand, maskT]), (KsT[:, kc * 128:(kc + 1) * 128], qq[:, h, :], [KsT, qq])]
                                    else:
                                        o = kc * 128 - q0
                                        sel = [([[-1, 512]], 511 + o, 1)] if o < 0 else [([[1, 512]], -o, -1)]
                                        vv = Vw
                                        smm = [(KwT[:, kc * 128:(kc + 1) * 128], qq[:, h, :], [KwT, qq])]
                                    accm = [(o_a, (lambda e, t, o_a=o_a, kc=kc, st_=st_, sp_=sp_, vv=vv: e.matmul(o_a[:, :], vv[:, kc, :], t[:, :], start=st_, stop=sp_)), [vv])]
                                    post = None
                                    if kc == nks - 1:
                                        post = (lambda h=h, o_a=o_a, s_a=s_a, br=br: finalize(4 * g + h, br, o_a, s_a, gt, False))
                                    items.append(make_item(smm, (padbias[:, 0:1], padbias) if kc * 128 < SO else None, sel, accm, post, sum_acc=(s_a, st_)))
                        pipeline(items)
                    for h in range(4):
                        ob = obf[cnt["ob"] % 2]
                        cnt["ob"] += 1
                        fw.op("act", lambda e: e.copy(out=ob[:, :], in_=oacc[h][:, :]), reads=[oacc[h]], writes=[ob])
                        fw.dma("sp", OT[4 * g + h, :, qt * 512:(qt + 1) * 512], ob[:, :], reads=[ob], writes=[OT])
            fw.barrier()


    if 4 in phases:
        conv_step(len(conv_list))
        with ExitStack() as es:
            def sbp(name, shape, dt=F32):
                return TT(es.enter_context(nc.sbuf_tensor(name, list(shape), dt)), name)
            A1 = sbp("A1", [128, 16, 512], BF16); A2 = sbp("A2", [128, 16, 512], BF16)
            Mt = [sbp("Mt%d" % i, [128, 4, 512], BF16) for i in range(2)]
            xT = sbp("xT", [128, 16, 512]); U = sbp("U", [128, 32, 512], BF16)
            slabs = [sbp("slab%d" % i, [128, 8192], BF16) for i in range(2)]
            sli = [0]
            xtk = [sbp("xtk%d" % i, [128, D]) for i in range(2)]
            wpp = sbp("wpp", [128, 2, D], BF16); pT = sbp("pT", [128, 2, 512], BF16)
            ptok = [sbp("ptok%d" % i, [128, 256]) for i in range(2)]; ptb = [sbp("ptb%d" % i, [128, 256], BF16) for i in range(2)]
            gcols = sbp("gcols", [128, 3, 16])
            sq = [sbp("sq%d" % i, [128, 512], BF16) for i in range(3)]
            rstdb = sbp("rstdb", [128, 512]); tmpf = [sbp("tmpf%d" % i, [128, 512]) for i in range(2)]
            rl = [sbp("rl%d" % i, [128, 512], BF16) for i in range(2)]
            k4 = {"sq": 0, "tmp": 0, "rl": 0, "q": 0}
            for i, gsrc in enumerate((norm_mlp_g, norm_ple_g, norm_final_g)):
                fw.dma("sp", gcols[:, i, :], gsrc.rearrange("(c p) -> p c", p=128), writes=[gcols], allow_slow_non_contiguous=True)
            fw.dma("sp", wpp[:, :, :], Wb["pp"][:, :].rearrange("(kc p) n -> p kc n", p=128), reads=[Wb["pp"]], writes=[wpp])

            def load_slab4(wb, r0, nk, c0, ncols):
                sl_ = slabs[sli[0] % 2]
                sli[0] += 1
                v = sl_.t[:, 0:nk * ncols].rearrange("p (k n) -> p k n", n=ncols)
                fw.dma("sp", v[:, :, :], wb[r0:r0 + nk * 128, c0:c0 + ncols].rearrange("(kc p) n -> p kc n", p=128), reads=[wb], writes=[sl_])
                return sl_, v

            def linear(wb, r0, nk, ncol_total, act_in, k_off, epilogue, slab_cols=512, col_off=0):
                for c0 in range(0, ncol_total, slab_cols):
                    sl_, v = load_slab4(wb, r0, nk, c0 + col_off, slab_cols)
                    for cc in range(slab_cols // 128):
                        p = ps_work()
                        for kc in range(nk):
                            fw.op("pe", lambda e: e.matmul(p[:, :], v[:, kc, cc * 128:(cc + 1) * 128], act_in[:, k_off + kc, :], start=(kc == 0), stop=(kc == nk - 1)),
                                  reads=[sl_, act_in], writes=[p])
                        epilogue(c0 // 128 + cc, p)

            def rms_to(gi, dst):
                pss = ACC[0]
                for n in range(16):
                    t = sq[k4["sq"] % 3]
                    k4["sq"] += 1
                    fw.op("act", lambda e: e.activation(out=t[:, :], in_=xT[:, n, :], func=AF.Square), reads=[xT], writes=[t])
                    fw.op("pe", lambda e: e.matmul(pss[:, :], onesb[:, :], t[:, :], start=(n == 0), stop=(n == 15)), reads=[onesb, t], writes=[pss])
                fw.op("act", lambda e: e.activation(out=rstdb[:, :], in_=pss[:, :], func=AF.Sqrt, scale=1.0 / D, bias=epsc[:, 0:1]), reads=[pss, epsc], writes=[rstdb])
                fw.op("dve", lambda e: e.reciprocal(out=rstdb[:, :], in_=rstdb[:, :]), reads=[rstdb], writes=[rstdb])
                for n in range(16):
                    fw.op("dve", lambda e: e.scalar_tensor_tensor(out=dst[:, n, :], in0=xT[:, n, :], scalar=gcols[:, gi, n:n + 1], in1=rstdb[:, :],
                                                                  op0=ALU.mult, op1=ALU.mult), reads=[xT, gcols, rstdb], writes=[dst])

            for qt in range(NOWN):
                t0_ = qt * 512
                for (src, wname, gsrc, first) in ((ZaT, "lru", MAT, True), (OT, "att", MBT, False)):
                    fw.dma("sp", A1[:, :, :], src[:, :, t0_:t0_ + 512].rearrange("n p t -> p n t"), reads=[src], writes=[A1])
                    mt_cur = {}

                    def epi(n, p, gsrc=gsrc, first=first, mt_cur=mt_cur):
                        if n % 4 == 0:
                            m = Mt[(n // 4) % 2]
                            fw.dma("sp", m[:, :, :], gsrc[n:n + 4, :, t0_:t0_ + 512].rearrange("n p t -> p n t"), reads=[gsrc], writes=[m])
                            mt_cur["m"] = m
                        m = mt_cur["m"]
                        if first:
                            fw.op("dve", lambda e: e.tensor_tensor(out=A2[:, n, :], in0=p[:, :], in1=m[:, n % 4, :], op=ALU.mult), reads=[p, m], writes=[A2])
                        else:
                            tf = tmpf[k4["tmp"] % 2]
                            k4["tmp"] += 1
                            fw.op("dve", lambda e: e.tensor_tensor(out=tf[:, :], in0=p[:, :], in1=m[:, n % 4, :], op=ALU.mult), reads=[p, m], writes=[tf])
                            fw.op("pool", lambda e: e.tensor_tensor(out=A2[:, n, :], in0=A2[:, n, :], in1=tf[:, :], op=ALU.add), reads=[A2, tf], writes=[A2])
                    linear(Wb[wname], 0, 16, D, A1, 0, epi)
                for tb in range(4):
                    xk = xtk[tb % 2]
                    fw.dma("sp", xk[:, :], xin[SO + t0_ + tb * 128: SO + t0_ + (tb + 1) * 128, :], writes=[xk])
                    for cg in range(4):
                        p = ps_work()
                        for cc in range(4):
                            c = 4 * cg + cc
                            fw.op("pe", lambda e: e.transpose(p[:, cc * 128:(cc + 1) * 128], xk[:, c * 128:(c + 1) * 128], identf[:, :]), reads=[xk, identf], writes=[p])
                        copy_op(evac_engine(), xT[:, 4 * cg:4 * cg + 4, tb * 128:(tb + 1) * 128], p[:, :].rearrange("p (a b) -> p a b", a=4), [p], [xT])
                def epi_add(n, p):
                    fw.op("dve", lambda e: e.tensor_tensor(out=xT[:, n, :], in0=xT[:, n, :], in1=p[:, :], op=ALU.add), reads=[xT, p], writes=[xT])
                linear(Wb["o"], 0, 16, D, A2, 0, epi_add)
                rms_to(0, A1)
                for half in range(2):
                    def epi_up(n, p):
                        r_ = rl[k4["rl"] % 2]
                        k4["rl"] += 1
                        fw.op("act", lambda e: e.activation(out=r_[:, :], in_=p[:, :], func=AF.Relu), reads=[p], writes=[r_])
                        fw.op("pool", lambda e: e.tensor_tensor(out=U[:, n, :], in0=r_[:, :], in1=r_[:, :], op=ALU.mult), reads=[r_], writes=[U])
                    linear(Wb["up"], 0, 16, 4096, A1, 0, epi_up, col_off=half * 4096)
                    linear(Wb["dn"], half * 4096, 32, D, U, 0, epi_add, slab_cols=256)
                rms_to(1, A1)
                for tb in range(4):
                    pk = ptok[tb % 2]; pb_ = ptb[tb % 2]
                    fw.dma("sp", pk[:, :], pin[t0_ + tb * 128:t0_ + (tb + 1) * 128, :], writes=[pk])
                    fw.op("act", lambda e: e.copy(out=pb_[:, :], in_=pk[:, :]), reads=[pk], writes=[pb_])
                    p = ps_work()
                    ppb = p.t[:, :].bitcast(BF16)
                    for pc in range(2):
                        fw.op("pe", lambda e: e.transpose(ppb[:, pc * 128:(pc + 1) * 128], pb_[:, pc * 128:(pc + 1) * 128], identb[:, :]), reads=[pb_, identb], writes=[p])
                    fw.op("dve", lambda e: e.tensor_copy(out=pT[:, :, tb * 128:(tb + 1) * 128], in_=ppb[:, 0:256].rearrange("p (a b) -> p a b", a=2)), reads=[p], writes=[pT])

                def epi_ple(n, p):
                    tf = tmpf[k4["tmp"] % 2]
                    k4["tmp"] += 1
                    fw.op("act", lambda e: e.activation(out=tf[:, :], in_=p[:, :], func=AF.Sigmoid), reads=[p], writes=[tf])
                    p2 = ps_work()
                    for pc in range(2):
                        fw.op("pe", lambda e: e.matmul(p2[:, :], wpp[:, pc, n * 128:(n + 1) * 128], pT[:, pc, :], start=(pc == 0), stop=(pc == 1)), reads=[wpp, pT], writes=[p2])
                    fw.op("dve", lambda e: e.tensor_tensor(out=tf[:, :], in0=tf[:, :], in1=p2[:, :], op=ALU.mult), reads=[tf, p2], writes=[tf])
                    fw.op("pool", lambda e: e.tensor_tensor(out=xT[:, n, :], in0=xT[:, n, :], in1=tf[:, :], op=ALU.add), reads=[xT, tf], writes=[xT])
                linear(Wb["pg"], 0, 16, D, A1, 0, epi_ple)
                rms_to(2, xT)
                for tb in range(4):
                    xk = xtk[tb % 2]
                    for cg in range(4):
                        p = ps_work()
                        for cc in range(4):
                            c = 4 * cg + cc
                            fw.op("pe", lambda e: e.transpose(p[:, cc * 128:(cc + 1) * 128], xT[:, c, tb * 128:(tb + 1) * 128], identf[:, :]), reads=[xT, identf], writes=[p])
                        copy_op(evac_engine(), xk[:, cg * 512:(cg + 1) * 512], p[:, :], [p], [xk])
                    fw.dma("sp", out[t0_ + tb * 128:t0_ + (tb + 1) * 128, :], xk[:, :], reads=[xk], writes=[outbuf])
            fw.barrier()

    fw.finish("sp")
    return nc, fw


def make_consts(c, NT):
    S = NT * 512; SO = S // 2
    NCB = S // 16 - 1
    o = {}
    o["c_ident"] = np.eye(128, dtype=np.float32)
    o["c_flag"] = np.full((128, 1), float(c), np.float32)
    o["c_padbias"] = np.full((128, 1), 0.0 if c else -30000.0, np.float32)
    fc = np.zeros(128, np.float32)
    f0 = 0 if c else SO // 64
    fc[:f0] = -2e4
    fc[f0] = 1e4
    o["c_fc"] = np.ascontiguousarray(np.broadcast_to(fc[None, :], (128, 128)))
    qq = np.arange(128)[:, None]; rp = np.arange(256)[None, :]
    r = rp - 128 - qq // 64
    o["c_gext"] = np.where((r == 0) | (r == -1), 1e4, np.where(r > 0, -1e4, 0.0)).astype(np.float32)
    ov = np.zeros((512, 129), np.float32)
    cb = np.arange(512)[:, None]; j = np.arange(128)[None, :]
    ov[:, :128] = ((16 * cb < 64 * j + 64) & (64 * j < 16 * cb + 32))
    ov[:, 128] = 1.0
    ov[NCB:, :] = 0.0
    o["c_ov"] = ov
    kk = np.arange(64 * 128)[None, :]; jj = np.arange(128)[:, None]
    o["c_expand"] = (kk // 64 == jj).astype(np.float32)
    ii = np.arange(48 * 128)[None, :] // 128; rr = np.arange(48)[:, None]
    o["c_gsel"] = (ii == rr).astype(np.float32)
    blk = np.arange(4)[None, :] * 128 + np.arange(128)[:, None]
    o["c_cmpbias"] = np.where((blk < SO // 16) & (c == 0), -30000.0, 0.0).astype(np.float32)
    return o


_NC_CACHE = {}


def kernel(**inputs):
    NT = 16
    S = NT * 512
    SO = S // 2
    x = np.asarray(inputs["x"], dtype=np.float32)
    p = np.asarray(inputs["p"], dtype=np.float32)
    B = x.shape[0]
    if "nc" not in _NC_CACHE:
        _NC_CACHE["nc"] = build(NT, dbg=False)[0]
    nc = _NC_CACHE["nc"]
    wnames = ["norm_mix_g", "w_in", "b_in", "conv_w", "conv_b", "w_gate_a", "b_gate_a", "w_gate_x", "b_gate_x", "lru_lambda",
              "w_lru_out", "cmp_pos_k", "cmp_w1_k", "cmp_w2_k", "cmp_pos_v", "cmp_w1_v", "cmp_w2_v", "w_attn_out", "w_o",
              "norm_mlp_g", "w_up", "w_down", "norm_ple_g", "w_ple_gate", "w_ple_proj"]
    W = {n: np.ascontiguousarray(np.asarray(inputs[n], dtype=np.float32)[0]) for n in wnames}
    W["norm_final_g"] = np.ascontiguousarray(np.asarray(inputs["norm_final_g"], dtype=np.float32))
    consts = [make_consts(0, NT), make_consts(1, NT)]
    in_maps = []
    for core in range(8):
        b, c = core // 2, core % 2
        if c == 1:
            xin = np.ascontiguousarray(x[b])
        else:
            xin = np.concatenate([np.zeros((SO, D), np.float32), x[b, :SO]], axis=0)
        m = dict(W)
        m["xin"] = xin
        m["pin"] = np.ascontiguousarray(p[0, b, c * SO:(c + 1) * SO])
        m.update(consts[c])
        in_maps.append(m)
    res = run_bass_kernel_spmd(nc, in_maps, core_ids=list(range(8)))
    out = np.empty((B, S, D), np.float32)
    for core in range(8):
        b, c = core // 2, core % 2
        out[b, c * SO:(c + 1) * SO] = np.asarray(res.results[core]["out"], dtype=np.float32)
    return out
```
